# Optimizing a Trainium2 kernel written in Bass

```python
import math
import jax, jax.numpy as jnp
from jax import lax
import numpy as np

D_MODEL = 2048
BATCH = 4
SEQ = 2048
DEPTH = 2

N_META = 16
CHUNK = 128
PAD = CHUNK - N_META
A_HEAD_DIM = 64
A_V_DIM = 2 * A_HEAD_DIM
A_HEADS = D_MODEL // A_V_DIM
A_QK = A_HEADS * 2 * A_HEAD_DIM
A_VW = A_HEADS * A_V_DIM
R_QK_DIM = 128
R_V_DIM = 2 * R_QK_DIM
R_HEADS = D_MODEL // R_V_DIM
R_QK = R_HEADS * R_QK_DIM
R_VW = R_HEADS * R_V_DIM
D_FF = 128 * ((8 * D_MODEL // 3 + 127) // 128)
CONV_W = 3
ROPE_THETA = 10000.0
ALPHA = (2 * DEPTH) ** 0.25
BETA = (8 * DEPTH) ** -0.25
EPS = 1e-5
NEG = -1e30

kernel_name = "hybrid_diffattn_retention_convffn_deepnorm"


def _split_points():
    sizes = (A_QK, A_QK, A_VW, R_QK, R_QK, R_VW, R_VW, D_MODEL, D_MODEL)
    return tuple(int(s) for s in np.cumsum(sizes)[:-1])


def _w_in_cols():
    return 2 * A_QK + A_VW + 2 * R_QK + 2 * R_VW + 2 * D_MODEL


def layer_norm(x, g, b):
    xf = x.astype(jnp.float32)
    mu = jnp.mean(xf, axis=-1, keepdims=True)
    var = jnp.mean(jnp.square(xf - mu), axis=-1, keepdims=True)
    y = (xf - mu) * lax.rsqrt(var + EPS)
    return (y * g.astype(jnp.float32) + b.astype(jnp.float32)).astype(x.dtype)


def rms_norm(x, g):
    xf = x.astype(jnp.float32)
    y = xf * lax.rsqrt(jnp.mean(jnp.square(xf), axis=-1, keepdims=True) + EPS)
    return (y * g.astype(jnp.float32)).astype(x.dtype)


def head_group_norm(x):
    xf = x.astype(jnp.float32)
    mu = jnp.mean(xf, axis=-1, keepdims=True)
    var = jnp.mean(jnp.square(xf - mu), axis=-1, keepdims=True)
    return ((xf - mu) * lax.rsqrt(var + EPS)).astype(x.dtype)


def rotary(t, pos, inv_freq):
    ang = pos.astype(jnp.float32)[:, None] * inv_freq[None, :]
    ang = jnp.concatenate([ang, ang], axis=-1)
    t1, t2 = jnp.split(t, 2, axis=-1)
    rot = jnp.concatenate([-t2, t1], axis=-1)
    return (t * jnp.cos(ang) + rot * jnp.sin(ang)).astype(t.dtype)


def diff_attention(q1, q2, k1, k2, v, lam):
    B, H, L, d = q1.shape
    nb = L // CHUNK
    scale = d ** -0.5
    kidx = jnp.arange(L)
    key_ok = kidx >= PAD

    def block(args):
        i, qb1, qb2 = args
        qidx = i * CHUNK + jnp.arange(CHUNK)
        allowed = (kidx[None, :] <= qidx[:, None]) & (key_ok[None, :] | (kidx[None, :] == qidx[:, None]))

        def probs(qb, k):
            s = jnp.einsum('bhqd,bhkd->bhqk', qb, k).astype(jnp.float32) * scale
            return jax.nn.softmax(jnp.where(allowed, s, NEG), axis=-1)

        a = probs(qb1, k1) - lam * probs(qb2, k2)
        return jnp.einsum('bhqk,bhkv->bhqv', a.astype(v.dtype), v)

    qb1 = q1.reshape(B, H, nb, CHUNK, d).transpose(2, 0, 1, 3, 4)
    qb2 = q2.reshape(B, H, nb, CHUNK, d).transpose(2, 0, 1, 3, 4)
    o = lax.map(block, (jnp.arange(nb), qb1, qb2))
    return o.transpose(1, 2, 0, 3, 4).reshape(B, H, L, v.shape[-1])


def retention(q, k, v, log_gamma):
    B, H, L, dk = q.shape
    dv = v.shape[-1]
    nc = L // CHUNK
    qc = q.reshape(B, H, nc, CHUNK, dk)
    kc = k.reshape(B, H, nc, CHUNK, dk)
    vc = v.reshape(B, H, nc, CHUNK, dv)
    p = jnp.arange(CHUNK, dtype=jnp.float32)
    diff = p[:, None] - p[None, :]
    lg = log_gamma[:, None, None]
    decay = jnp.where(diff >= 0, jnp.exp(lg * jnp.maximum(diff, 0.0)), 0.0)
    s = jnp.einsum('bhcnk,bhcmk->bhcnm', qc, kc) * decay[None, :, None]
    o_intra = jnp.einsum('bhcnm,bhcmv->bhcnv', s.astype(v.dtype), vc)
    k_tail = kc * jnp.exp(log_gamma[:, None] * (CHUNK - 1 - p))[None, :, None, :, None]
    kv = jnp.einsum('bhcmk,bhcmv->cbhkv', k_tail, vc).astype(jnp.float32)
    chunk_decay = jnp.exp(log_gamma * CHUNK)[None, :, None, None]

    def step(state, kv_c):
        return chunk_decay * state + kv_c, state

    _, prev = lax.scan(step, jnp.zeros((B, H, dk, dv), jnp.float32), kv)
    q_dec = qc * jnp.exp(log_gamma[:, None] * (p + 1.0))[None, :, None, :, None]
    o_cross = jnp.einsum('bhcnk,cbhkv->bhcnv', q_dec, prev)
    return (o_intra + o_cross).astype(v.dtype).reshape(B, H, L, dv)


def hybrid_mixer(h, pos, valid, layer, w_in, lam_q1, lam_k1, lam_q2, lam_k2,
                 subln_g, ret_gn_g, w_branch_a, w_branch_b, w_out):
    B, L, _ = h.shape
    proj = h @ w_in
    aq, ak, av, rq, rk, rv, rg, ga, gb = jnp.split(proj, _split_points(), axis=-1)

    a_freq = 1.0 / (ROPE_THETA ** (jnp.arange(0, A_HEAD_DIM, 2, dtype=jnp.float32) / A_HEAD_DIM))

    def pair_heads(t):
        return t.reshape(B, L, A_HEADS, 2, A_HEAD_DIM).transpose(0, 2, 3, 1, 4)

    qa = rotary(pair_heads(aq), pos, a_freq)
    ka = rotary(pair_heads(ak), pos, a_freq)
    va = av.reshape(B, L, A_HEADS, A_V_DIM).transpose(0, 2, 1, 3)
    lam_init = 0.8 - 0.6 * math.exp(-0.3 * layer)
    lam = (jnp.exp(jnp.sum(lam_q1.astype(jnp.float32) * lam_k1.astype(jnp.float32)))
           - jnp.exp(jnp.sum(lam_q2.astype(jnp.float32) * lam_k2.astype(jnp.float32))) + lam_init)
    o_a = diff_attention(qa[:, :, 0], qa[:, :, 1], ka[:, :, 0], ka[:, :, 1], va, lam)
    o_a = rms_norm(o_a, subln_g) * (1.0 - lam_init)
    branch_a = o_a.transpose(0, 2, 1, 3).reshape(B, L, A_VW) @ w_branch_a

    r_freq = 1.0 / (ROPE_THETA ** jnp.linspace(0.0, 1.0, R_QK_DIM // 2, dtype=jnp.float32))

    def heads(t, dh):
        return t.reshape(B, L, R_HEADS, dh).transpose(0, 2, 1, 3)

    qr = rotary(heads(rq, R_QK_DIM), pos, r_freq)
    kr = rotary(heads(rk, R_QK_DIM), pos, r_freq) * (R_QK_DIM ** -0.5) * valid[None, None, :, None]
    vr = heads(rv, R_V_DIM)
    log_gamma = jnp.log(1.0 - 2.0 ** (-5.0 - jnp.arange(R_HEADS, dtype=jnp.float32)))
    o_r = head_group_norm(retention(qr, kr, vr, log_gamma))
    o_r = o_r.transpose(0, 2, 1, 3).reshape(B, L, R_VW) * ret_gn_g
    branch_b = (o_r * jax.nn.silu(rg)) @ w_branch_b

    merged = jax.nn.sigmoid(ga) * branch_a + jax.nn.sigmoid(gb) * branch_b
    return merged @ w_out


def conv_ffn(h, valid, w_up, conv_w, conv_b, w_down):
    L = h.shape[1]
    u = (h @ w_up) * valid[None, :, None]
    up = jnp.pad(u, ((0, 0), (CONV_W - 1, 0), (0, 0)))
    c = sum(up[:, j:j + L] * conv_w[j] for j in range(CONV_W)) + conv_b
    gate, val = jnp.split(c, 2, axis=-1)
    return (jax.nn.gelu(gate, approximate=False) * val) @ w_down


def setup_inputs(seed: int = 0) -> dict:
    key = jax.random.key(seed)
    ks = jax.random.split(key, 22)
    f32 = jnp.float32
    nrm = lambda k, shape, s: jax.random.normal(k, shape, f32) * s
    cols = _w_in_cols()
    col_scale = jnp.concatenate([
        jnp.ones((2 * A_QK,), f32), jnp.full((A_VW,), BETA, f32),
        jnp.ones((2 * R_QK,), f32), jnp.full((R_VW,), BETA, f32),
        jnp.ones((R_VW + 2 * D_MODEL,), f32)])
    return {
        "x": nrm(ks[0], (BATCH, SEQ, D_MODEL), 1.0),
        "meta_tokens": nrm(ks[1], (N_META, D_MODEL), 1.0),
        "ln_emb_g": 1.0 + nrm(ks[2], (D_MODEL,), 0.02),
        "ln_emb_b": nrm(ks[3], (D_MODEL,), 0.02),
        "w_in": nrm(ks[4], (DEPTH, D_MODEL, cols), D_MODEL ** -0.5) * col_scale,
        "lam_q1": nrm(ks[5], (DEPTH, A_HEAD_DIM), 0.1),
        "lam_k1": nrm(ks[6], (DEPTH, A_HEAD_DIM), 0.1),
        "lam_q2": nrm(ks[7], (DEPTH, A_HEAD_DIM), 0.1),
        "lam_k2": nrm(ks[8], (DEPTH, A_HEAD_DIM), 0.1),
        "subln_g": 1.0 + nrm(ks[9], (DEPTH, A_V_DIM), 0.02),
        "ret_gn_g": 1.0 + nrm(ks[10], (DEPTH, R_VW), 0.02),
        "w_branch_a": nrm(ks[11], (DEPTH, A_VW, D_MODEL), BETA * A_VW ** -0.5),
        "w_branch_b": nrm(ks[12], (DEPTH, R_VW, D_MODEL), BETA * R_VW ** -0.5),
        "w_out": nrm(ks[13], (DEPTH, D_MODEL, D_MODEL), BETA * D_MODEL ** -0.5),
        "ln1_g": 1.0 + nrm(ks[14], (DEPTH, D_MODEL), 0.02),
        "ln1_b": nrm(ks[15], (DEPTH, D_MODEL), 0.02),
        "w_up": nrm(ks[16], (DEPTH, D_MODEL, 2 * D_FF), D_MODEL ** -0.5),
        "conv_w": nrm(ks[17], (DEPTH, CONV_W, 2 * D_FF), CONV_W ** -0.5),
        "conv_b": nrm(ks[18], (DEPTH, 2 * D_FF), 0.02),
        "w_down": nrm(ks[19], (DEPTH, D_FF, D_MODEL), BETA * D_FF ** -0.5),
        "ln2_g": 1.0 + nrm(ks[20], (DEPTH, D_MODEL), 0.02),
        "ln2_b": nrm(ks[21], (DEPTH, D_MODEL), 0.02),
    }


def reference(x, meta_tokens, ln_emb_g, ln_emb_b, w_in, lam_q1, lam_k1, lam_q2, lam_k2,
              subln_g, ret_gn_g, w_branch_a, w_branch_b, w_out, ln1_g, ln1_b,
              w_up, conv_w, conv_b, w_down, ln2_g, ln2_b):
    B = x.shape[0]
    filler = jnp.zeros((B, PAD, D_MODEL), x.dtype)
    meta = jnp.broadcast_to(meta_tokens.astype(x.dtype)[None], (B, N_META, D_MODEL))
    h = jnp.concatenate([filler, meta, x], axis=1)
    L = h.shape[1]
    idx = jnp.arange(L)
    pos = idx - PAD
    valid = (idx >= PAD).astype(x.dtype)
    h = layer_norm(h, ln_emb_g, ln_emb_b)
    for l in range(DEPTH):
        mix = hybrid_mixer(h, pos, valid, l, w_in[l], lam_q1[l], lam_k1[l], lam_q2[l], lam_k2[l],
                           subln_g[l], ret_gn_g[l], w_branch_a[l], w_branch_b[l], w_out[l])
        h = layer_norm(ALPHA * h + mix, ln1_g[l], ln1_b[l])
        ffn = conv_ffn(h, valid, w_up[l], conv_w[l], conv_b[l], w_down[l])
        h = layer_norm(ALPHA * h + ffn, ln2_g[l], ln2_b[l])
    return h[:, PAD + N_META:]
```

```python
import math
import numpy as np
import ml_dtypes
import concourse.bass as bass
import concourse.mybir as mybir
from concourse.bass_utils import run_bass_kernel_spmd

F32 = mybir.dt.float32
BF16 = mybir.dt.bfloat16
AF = mybir.ActivationFunctionType
ALU = mybir.AluOpType

D = 2048
DFF = 5504
NFC = 43
EPS = 1e-5
N_META = 16
PAD = 112
ROPE_THETA = 10000.0


def mk(name, *args, **kw):
    return lambda e: getattr(e, name)(*args, **kw)


class Buf:
    __slots__ = ("name", "lw", "rd")

    def __init__(self, name):
        self.name = name
        self.lw = None
        self.rd = []


class Eng:
    def __init__(self, prog, name, sems, is_dma=False):
        self.prog = prog
        self.name = name
        self.sems = sems
        self.is_dma = is_dma
        self.ops = []
        self.cnt = 0
        self.waited = {}
        self.pend_r = []
        self.pend_w = []

    def _wait(self, ev):
        if ev is None:
            return
        si, val = ev
        if self.waited.get(si, 0) >= val:
            return
        self.waited[si] = val
        sem = self.prog.semlist[si]
        self.ops.append(mk("wait_ge", sem, val))

    def _deps(self, reads, writes):
        for b in reads:
            self._wait(b.lw)
        for b in writes:
            self._wait(b.lw)
            for ev in b.rd:
                self._wait(ev)

    def op(self, fn, reads=(), writes=(), signal=True, after=()):
        self._deps(reads, writes)
        for b in after:
            for ev in b.rd:
                self._wait(ev)
        if self.is_dma:
            K = len(self.sems)
            n = self.cnt
            self.cnt += 1
            slot = n % K
            need = 16 * (n // K)
            si = self.sems[slot]
            self._wait((si, need))
            ev = (si, need + 16)
            sem = self.prog.semlist[si]
            self.ops.append(lambda e, fn=fn, sem=sem: fn(e).then_inc(sem, 16))
        elif signal:
            self.cnt += 1
            si = self.sems[0]
            ev = (si, self.cnt)
            sem = self.prog.semlist[si]
            self.ops.append(lambda e, fn=fn, sem=sem: fn(e).then_inc(sem, 1))
        else:
            self.ops.append(lambda e, fn=fn: fn(e))
            self.pend_r.extend(reads)
            self.pend_w.extend(writes)
            return None
        for b in list(reads) + self.pend_r:
            b.rd.append(ev)
        for b in list(writes) + self.pend_w:
            b.lw = ev
            b.rd = []
        self.pend_r = []
        self.pend_w = []
        return ev

    def last_event(self):
        if self.is_dma:
            evs = []
            K = len(self.sems)
            for slot in range(K):
                cnt = (self.cnt - slot + K - 1) // K if self.cnt > slot else 0
                if cnt > 0:
                    evs.append((self.sems[slot], 16 * cnt))
            return evs
        if self.cnt == 0:
            return []
        return [(self.sems[0], self.cnt)]


class Prog:
    def __init__(self, nc):
        self.nc = nc
        self.semlist = []

        def mk(n, name):
            ids = []
            for i in range(n):
                self.semlist.append(nc.alloc_semaphore(f"s_{name}{i}"))
                ids.append(len(self.semlist) - 1)
            return ids

        self.pe = Eng(self, "pe", mk(1, "pe"))
        self.act = Eng(self, "act", mk(1, "act"))
        self.dve = Eng(self, "dve", mk(1, "dve"))
        self.pool = Eng(self, "pool", mk(1, "pool"))
        self.sp = Eng(self, "sp", mk(12, "sp"), is_dma=True)
        self.engs = [self.pe, self.act, self.dve, self.pool, self.sp]

    def barrier(self):
        evs = []
        for e in self.engs:
            assert not e.pend_r and not e.pend_w, e.name
            evs.extend(e.last_event())
        for e in self.engs:
            for ev in evs:
                e._wait(ev)

    def finish(self):
        self.barrier()
        nc = self.nc
        with nc.Block() as block:
            @block.sync
            def _(e):
                for f in self.sp.ops:
                    f(e)

            @block.tensor
            def _(e):
                for f in self.pe.ops:
                    f(e)

            @block.scalar
            def _(e):
                for f in self.act.ops:
                    f(e)

            @block.vector
            def _(e):
                for f in self.dve.ops:
                    f(e)

            @block.gpsimd
            def _(e):
                for f in self.pool.ops:
                    f(e)


def make_consts(nch):
    L = nch * 128
    pos = (np.arange(L) - PAD).astype(np.float32)
    a_freq = (1.0 / (ROPE_THETA ** (np.arange(0, 64, 2, dtype=np.float32) / 64))).astype(np.float32)
    r_freq = (1.0 / (ROPE_THETA ** np.linspace(0.0, 1.0, 64, dtype=np.float32))).astype(np.float32)
    angA = (pos[:, None] * a_freq[None, :]).astype(np.float32)
    angR = (pos[:, None] * r_freq[None, :]).astype(np.float32)
    rope = np.concatenate([np.cos(angA), np.sin(angA), np.cos(angR), np.sin(angR)], axis=1).astype(np.float32)
    p = np.arange(128, dtype=np.float64)
    lg = np.log(1.0 - 2.0 ** (-5.0 - np.arange(8, dtype=np.float64)))
    qdec = np.exp(lg[None, :] * (p[:, None] + 1.0))
    kinv = np.exp(-lg[None, :] * (p[:, None] + 1.0)) * (128.0 ** -0.5)
    valid = (p >= PAD).astype(np.float64)[:, None]
    kinv0 = kinv * valid
    g128 = np.broadcast_to(np.exp(lg * 128.0)[None, :], (128, 8))
    cst = np.concatenate([qdec, kinv, kinv0, g128, valid, np.full((128, 1), EPS), np.full((128, 1), 1e-30), np.full((128, 1), -0.5)], axis=1).astype(np.float32)
    ident = np.eye(128, dtype=np.float32)
    tri = (np.arange(128)[:, None] <= np.arange(128)[None, :]).astype(np.float32)
    return {
        "rope": rope,
        "cst": cst,
        "ident_bf": ident.astype(ml_dtypes.bfloat16),
        "ident_f": ident,
        "tri_bf": tri.astype(ml_dtypes.bfloat16),
    }


CST_QDEC, CST_KINV, CST_KINV0, CST_G128, CST_VALID, CST_EPS, CST_TINY, CST_NHALF = 0, 8, 16, 24, 32, 33, 34, 35
CST_N = 36


def build_program(nch, tiles, depth, debug=None):
    assert sum(tiles) == nch
    L = nch * 128
    NTM = max(tiles)
    TM = NTM * 128
    alpha = float((2 * depth) ** 0.25)

    nc = bass.Bass("TRN2", target_bir_lowering=False)
    P = Prog(nc)
    import os
    STOP = int(os.environ.get("KSTOP", "99"))

    class StopBuild(Exception):
        pass

    def checkpoint(k):
        if STOP <= k:
            raise StopBuild()
    pe, act, dve, pool, sp = P.pe, P.act, P.dve, P.pool, P.sp

    def din(name, shape, dt=F32):
        return nc.dram_tensor(name, list(shape), dt, kind="ExternalInput").ap()

    def dscr(name, shape, dt=BF16):
        return nc.dram_tensor(name, list(shape), dt, kind="Internal").ap()

    xin = din("xin", [L, D])
    w_in = din("w_in", [depth, D, 16384])
    w_a = din("w_a", [depth, D, D])
    w_b = din("w_b", [depth, D, D])
    w_o = din("w_o", [depth, D, D])
    w_up = din("w_up", [depth, D, 2 * DFF])
    w_dn = din("w_dn", [depth, DFF, D])
    lnp = din("lnp", [2 + 4 * depth, D])
    retg = din("retg", [depth, D])
    subg = din("subg", [depth, 128])
    lamv = din("lamv", [depth, 4, 64])
    convw = din("convw", [depth, 4, 2 * NFC, 128])
    rope_d = din("rope", [L, 192])
    cst_d = din("cst", [128, CST_N])
    identbf_d = din("ident_bf", [128, 128], BF16)
    identf_d = din("ident_f", [128, 128])
    tribf_d = din("tri_bf", [128, 128], BF16)
    out_d = nc.dram_tensor("out", [L, D], F32, kind="ExternalOutput").ap()

    qT_s = dscr("qT_s", [16, 128, TM])
    kT_c = dscr("kT_c", [depth, 16, 128, L])
    v_c = dscr("v_c", [depth, 16, 128, nch * 129])
    qdT_s = dscr("qdT_s", [8, 128, TM])
    kiT_s = dscr("kiT_s", [8, 128, TM])
    ki_s = dscr("ki_s", [TM, 1024])
    rv_s = dscr("rv_s", [TM, D])
    sg_s = dscr("sg_s", [TM, D])
    sga_s = dscr("sga_s", [TM, D])
    sgb_s = dscr("sgb_s", [TM, D])
    UPL = 44 + 3 * ((NFC + 3) // 4)
    NUP = depth * UPL
    wcache_l = [dscr(f"wcache{l}", [UPL, 128, 8192]) for l in range(depth)]
    B_wc = [Buf(f"wc{i}") for i in range(NUP)]
    B_qT, B_kT, B_v = Buf("qT_s"), Buf("kT_c"), Buf("v_c")
    B_qdT, B_kiT, B_ki, B_rv, B_sg, B_sga, B_sgb = (Buf(n) for n in ["qdT", "kiT", "ki", "rv", "sg", "sga", "sgb"])

    dbg = {}
    if debug:
        for name, shape in debug.items():
            dbg[name] = nc.dram_tensor("dbg_" + name, list(shape), F32, kind="ExternalOutput").ap()

    def sb(name, shape, dt):
        return nc.alloc_sbuf_tensor("sb_" + name, list(shape), dt)

    h = sb("h", [128, NTM, D], F32)
    B_h = [Buf(f"h{c}") for c in range(NTM)]
    bufA = sb("bufA", [128, 16 * TM], BF16)
    bufB = sb("bufB", [128, 16 * TM], BF16)
    B_A, B_B = Buf("bufA"), Buf("bufB")
    NST = 4
    wst = [sb(f"wst{i}", [128, 2048], F32) for i in range(NST)]
    B_wst = [Buf(f"wst{i}") for i in range(NST)]
    wbf = [sb(f"wbf{i}", [128, 8192], BF16) for i in range(2)]
    B_wbf = [[Buf(f"wbf{i}_{q}") for q in range(4)] for i in range(2)]
    state = sb("state", [128, depth, 8, 256], F32)
    B_state = Buf("state")
    cst = sb("cst", [128, CST_N], F32)
    ident_bf = sb("ident_bf", [128, 128], BF16)
    ident_f = sb("ident_f", [128, 128], F32)
    tri_bf = sb("tri_bf", [128, 128], BF16)
    B_const = Buf("const")
    cw = sb("cw", [128, depth, 4, 2 * NFC], F32)
    B_cw = Buf("cw")
    halo = sb("halo", [128, depth, 2 * NFC, 2], F32)
    B_halo = Buf("halo")
    subg_bc = sb("subg_bc", [128, depth, 128], F32)
    lam_t = sb("lam_t", [128, depth, 8], F32)
    rope_t = sb("rope_t", [128, NTM, 192], F32)
    B_rope = Buf("rope")
    B_lay = Buf("layerparams")
    SCR_BYTES = 35 * 1024
    scr = sb("scr", [128, SCR_BYTES // 2], BF16)

    psb = [nc.alloc_psum_tensor(f"ps{i}", [128, 512], F32) for i in range(8)]
    B_ps = [Buf(f"ps{i}") for i in range(8)]
    ps_groups = {"acc": [0, 1, 2, 3], "mm": [4, 5], "tp": [6, 7]}
    ps_rr = {"acc": 0, "mm": 0, "tp": 0}

    def next_ps(grp="acc"):
        lst = ps_groups[grp]
        i = lst[ps_rr[grp] % len(lst)]
        ps_rr[grp] += 1
        return psb[i], B_ps[i]

    class Carver:
        def __init__(self):
            self.off = 0

        def take(self, shape, dt):
            n = int(np.prod(shape[1:]))
            nb = n * (4 if dt == F32 else 2)
            nb = (nb + 31) // 32 * 32
            assert self.off + nb <= SCR_BYTES, (self.off, nb, SCR_BYTES)
            ap = scr[:, self.off // 2:(self.off + nb) // 2]
            self.off += nb
            if dt == F32:
                ap = ap.bitcast(F32)
            ap = ap[:, 0:n]
            if len(shape) == 3:
                ap = ap.rearrange("p (a b) -> p a b", b=shape[2])
            elif len(shape) == 4:
                ap = ap.rearrange("p (a b c) -> p a b c", b=shape[2], c=shape[3])
            return ap

    sp.op(mk("dma_start", out=cst[:], in_=cst_d), writes=[B_const])
    sp.op(mk("dma_start", out=ident_bf[:], in_=identbf_d), writes=[B_const])
    sp.op(mk("dma_start", out=ident_f[:], in_=identf_d), writes=[B_const])
    sp.op(mk("dma_start", out=tri_bf[:], in_=tribf_d), writes=[B_const])
    pool.op(mk("memset", state[:], 0.0), writes=[B_state])
    pool.op(mk("memset", halo[:], 0.0), writes=[B_halo])
    for l in range(depth):
        sp.op(mk("dma_start", out=subg_bc[:, l, :], in_=subg[l:l + 1, :].partition_broadcast(128)[:, 0, :]), writes=[B_lay])
    P.barrier()
    cv = Carver()
    lq = cv.take([128, depth, 4, 64], F32)
    lj = cv.take([128, 64], F32)
    for l in range(depth):
        sp.op(mk("dma_start", out=lq[:, l, :, :], in_=lamv[l:l + 1, :, :].partition_broadcast(128)[:, 0, :, :]), writes=[B_lay])
    for l in range(depth):
        lam_init = 0.8 - 0.6 * math.exp(-0.3 * l)
        for k in range(2):
            dve.op(mk("tensor_tensor", out=lj, in0=lq[:, l, 2 * k, :], in1=lq[:, l, 2 * k + 1, :], op=ALU.mult), reads=[B_lay], writes=[B_lay])
            dve.op(mk("tensor_reduce", out=lam_t[:, l, 1 + k:2 + k], in_=lj, op=ALU.add, axis=mybir.AxisListType.X), reads=[B_lay], writes=[B_lay])
        act.op(mk("activation", out=lam_t[:, l, 3:5], in_=lam_t[:, l, 1:3], func=AF.Exp), reads=[B_lay], writes=[B_lay])
        dve.op(mk("tensor_tensor", out=lam_t[:, l, 5:6], in0=lam_t[:, l, 4:5], in1=lam_t[:, l, 3:4], op=ALU.subtract), reads=[B_lay], writes=[B_lay])
        dve.op(mk("tensor_scalar", out=lam_t[:, l, 0:1], in0=lam_t[:, l, 5:6], scalar1=-lam_init, scalar2=None, op0=ALU.add), reads=[B_lay], writes=[B_lay])
        dve.op(mk("tensor_scalar", out=subg_bc[:, l, :], in0=subg_bc[:, l, :], scalar1=1.0 - lam_init, scalar2=None, op0=ALU.mult), reads=[B_lay], writes=[B_lay])
    cwl = cv.take([128, depth * 4, 128], F32)
    for l in range(depth):
        for j in range(4):
            sp.op(mk("dma_start", out=cwl[0:2 * NFC, l * 4 + j, :], in_=convw[l, j, :, :]), writes=[B_cw])
    for l in range(depth):
        for j in range(4):
            ps, bps = next_ps("tp")
            pe.op(mk("transpose", out=ps[:, 0:2 * NFC], in_=cwl[0:2 * NFC, l * 4 + j, :], identity=ident_f[0:2 * NFC, 0:2 * NFC]),
                  reads=[B_cw, B_const], writes=[bps])
            dve.op(mk("tensor_copy", out=cw[:, l, j, :], in_=ps[:, 0:2 * NFC]), reads=[bps], writes=[B_lay])
    P.barrier()

    units = []
    unit_state = {"emitted": 0, "stq": 0}
    cast_eng_cycle = [act, dve, act, dve]
    pending_casts = []

    def emit_unit_load(ui):
        u = units[ui]
        slot = ui % 2
        cidx = ui % NUP
        if ui >= NUP:
            sp.op(mk("dma_start", out=wbf[slot][:, :], in_=wcache_l[cidx // UPL][cidx % UPL, :, :]), reads=[B_wc[cidx]], writes=B_wbf[slot])
            return
        nq = len(u["quarters"])
        for qi, (src, ne, shp) in enumerate(u["quarters"]):
            si = unit_state["stq"] % NST
            unit_state["stq"] += 1
            st = wst[si]
            if len(shp) == 2:
                dst = st[:, 0:ne].rearrange("p (a b) -> p a b", b=shp[1])
            else:
                dst = st[:, 0:ne]
            sp.op(mk("dma_start", out=dst, in_=src), writes=[B_wst[si]])
            ce = cast_eng_cycle[qi % 4]
            o = wbf[slot][:, qi * ne: qi * ne + ne]
            i_ = st[:, 0:ne]

            def do_cast(ce=ce, o=o, i_=i_, si=si, slot=slot, qi=qi, nq=nq, cidx=cidx):
                if ce is act:
                    ce.op(mk("copy", out=o, in_=i_), reads=[B_wst[si]], writes=[B_wbf[slot][qi]], after=B_wbf[slot])
                else:
                    ce.op(mk("tensor_copy", out=o, in_=i_), reads=[B_wst[si]], writes=[B_wbf[slot][qi]], after=B_wbf[slot])
                if qi == nq - 1 and len(tiles) > 1:
                    sp.op(mk("dma_start", out=wcache_l[cidx // UPL][cidx % UPL, :, :], in_=wbf[slot][:, :]), reads=B_wbf[slot], writes=[B_wc[cidx]])
            pending_casts.append((ui, do_cast))

    def pump(n=1):
        for _ in range(n):
            if pending_casts:
                pending_casts.pop(0)[1]()

    def get_unit(ui):
        def flush():
            while pending_casts and pending_casts[0][0] <= ui:
                pending_casts.pop(0)[1]()
        flush()
        while unit_state["emitted"] <= ui:
            emit_unit_load(unit_state["emitted"])
            unit_state["emitted"] += 1
            flush()
        if unit_state["emitted"] == ui + 1 and ui + 1 < len(units):
            emit_unit_load(ui + 1)
            unit_state["emitted"] += 1
        return wbf[ui % 2], B_wbf[ui % 2]

    def unit_cols(wmat, l, col0, ncol):
        qs = []
        for q in range(4):
            src = wmat[l, q * 512:(q + 1) * 512, col0:col0 + ncol].rearrange("(k p) c -> p k c", p=128)
            qs.append((src, 4 * ncol, (4, ncol)))
        return {"quarters": qs, "kind": "cols", "ncol": ncol}

    def unit_rows(wmat, l, row0, nf):
        qs = []
        for f in range(nf):
            src = wmat[l, row0 + f * 128: row0 + (f + 1) * 128, :]
            qs.append((src, 2048, (2048,)))
        return {"quarters": qs, "kind": "rows", "nf": nf}

    ngrp = (NFC + 3) // 4
    for ti in range(len(tiles)):
        for l in range(depth):
            for u in range(32):
                units.append(unit_cols(w_in, l, u * 512, 512))
            for u in range(4):
                units.append(unit_cols(w_a, l, u * 512, 512))
            for u in range(4):
                units.append(unit_cols(w_b, l, u * 512, 512))
            for u in range(4):
                units.append(unit_cols(w_o, l, u * 512, 512))
            for g in range(ngrp):
                nf = min(4, NFC - 4 * g)
                units.append(unit_cols(w_up, l, g * 512, nf * 128))
                units.append(unit_cols(w_up, l, DFF + g * 512, nf * 128))
                if g > 0:
                    units.append(unit_rows(w_dn, l, (g - 1) * 512, 4))
            units.append(unit_rows(w_dn, l, (ngrp - 1) * 512, NFC - 4 * (ngrp - 1)))
    ucur = [0]

    def take_unit():
        ui = ucur[0]
        ucur[0] += 1
        t, b = get_unit(ui)
        return units[ui], t, b

    def gemm_tok(AT, B_AT, T, NT, W, B_W, ncol, epi):
        Wv = W[:, 0:16 * ncol].rearrange("p (k c) -> p k c", c=ncol)
        tails = []
        for c in range(NT):
            ps, bps = next_ps()
            for kc in range(16):
                pe.op(mk("matmul", ps[:, 0:ncol], lhsT=AT[:, kc, c * 128:(c + 1) * 128], rhs=Wv[:, kc, :],
                                                            start=(kc == 0), stop=(kc == 15)),
                      reads=[B_AT, B_W[kc // 4]], writes=[bps], signal=(kc == 15))
            pump(1)
            tl = epi(c, ps, bps)
            if tl is not None:
                tails.append(tl)
            if len(tails) > 2:
                tails.pop(0)()
        while tails:
            tails.pop(0)()

    def transpose_cols(src, B_src, nblk, dst_fn, B_dst, evac_engs):
        j = 0
        k = 0
        while j < nblk:
            n = min(8, nblk - j)
            ps, bps = next_ps("tp")
            pv = ps[:].bitcast(BF16).rearrange("p (a b) -> p a b", b=128)
            for jj in range(n):
                pe.op(mk("transpose", out=pv[:, jj, :], in_=src[:, (j + jj) * 128:(j + jj + 1) * 128], identity=ident_bf[:]),
                      reads=[B_src, B_const], writes=[bps], signal=(jj == n - 1))
            ee = evac_engs[k % len(evac_engs)]
            k += 1
            d = dst_fn(j, n)
            if ee is act:
                ee.op(mk("copy", out=d, in_=pv[:, 0:n, :]), reads=[bps], writes=[B_dst])
            else:
                ee.op(mk("tensor_copy", out=d, in_=pv[:, 0:n, :]), reads=[bps], writes=[B_dst])
            j += n

    def load_ln_params(cvr, row_g, row_b):
        g = cvr.take([128, D], F32)
        b = cvr.take([128, D], F32)
        B = Buf("lnp")
        sp.op(mk("dma_start", out=g, in_=lnp[row_g:row_g + 1, :].partition_broadcast(128)[:, 0, :]), writes=[B])
        sp.op(mk("dma_start", out=b, in_=lnp[row_b:row_b + 1, :].partition_broadcast(128)[:, 0, :]), writes=[B])
        return g, b, B

    def layer_norm_chunk(c, g, b, Bp, stats, mv, Bs):
        hc = h[:, c, :]
        for s in range(4):
            dve.op(mk("bn_stats", out=stats[:, s, :], in_=h[:, c, s * 512:(s + 1) * 512]), reads=[B_h[c]], writes=[Bs])
        dve.op(mk("bn_aggr", out=mv[:, 0:2], in_=stats.rearrange("p a b -> p (a b)")), reads=[Bs], writes=[Bs])
        act.op(mk("activation", out=mv[:, 2:3], in_=mv[:, 1:2], func=AF.Sqrt, bias=cst[:, CST_EPS:CST_EPS + 1], scale=1.0), reads=[Bs, B_const], writes=[Bs])
        dve.op(mk("reciprocal", out=mv[:, 3:4], in_=mv[:, 2:3]), reads=[Bs], writes=[Bs])
        dve.op(mk("scalar_tensor_tensor", out=hc, in0=hc, scalar=mv[:, 0:1], in1=g, op0=ALU.subtract, op1=ALU.mult), reads=[Bs, Bp, B_h[c]], writes=[B_h[c]])
        dve.op(mk("scalar_tensor_tensor", out=hc, in0=hc, scalar=mv[:, 3:4], in1=b, op0=ALU.mult, op1=ALU.add), reads=[Bs, Bp, B_h[c]], writes=[B_h[c]])

    def make_T_from_h(cvr, NT, T, dstT, B_dst):
        hb = [cvr.take([128, D], BF16) for _ in range(2)]
        Bhb = [Buf("hb0"), Buf("hb1")]
        for c in range(NT):
            k = c % 2
            act.op(mk("copy", out=hb[k], in_=h[:, c, :]), reads=[B_h[c]], writes=[Bhb[k]])
            transpose_cols(hb[k], Bhb[k], 16, lambda j0, n, c=c: dstT[:, j0:j0 + n, c * 128:(c + 1) * 128], B_dst, [dve, act])

    def dbg_dump(name, src_ap, rows, reads):
        if name in dbg:
            sp.op(mk("dma_start", out=dbg[name][rows], in_=src_ap), reads=reads)

    def main_loop():
      c0 = 0
      checkpoint(0)
      for ti, NT in enumerate(tiles):
          T = NT * 128
          nkc_tile = c0 + NT
          Lk = nkc_tile * 128
          AT = bufA[:, 0:16 * T].rearrange("p (k t) -> p k t", t=T)
          BT = bufB[:, 0:16 * T].rearrange("p (k t) -> p k t", t=T)
          A_tok = bufA[:, 0:NT * D].rearrange("p (c d) -> p c d", d=D)

          P.barrier()
          cv = Carver()
          for c in range(NT):
              sp.op(mk("dma_start", out=h[:, c, :], in_=xin[(c0 + c) * 128:(c0 + c + 1) * 128, :]), writes=[B_h[c]])
          sp.op(mk("dma_start", out=rope_t[:, 0:NT, :], in_=rope_d[c0 * 128:(c0 + NT) * 128, :].rearrange("(c p) f -> p c f", p=128)), writes=[B_rope])
          g_, b_, Bp = load_ln_params(cv, 0, 1)
          stats = cv.take([128, 4, 6], F32)
          mv = cv.take([128, 8], F32)
          Bs = Buf("lnstat")
          for c in range(NT):
              layer_norm_chunk(c, g_, b_, Bp, stats, mv, Bs)
          make_T_from_h(cv, NT, T, AT, B_A)
          if ti == 0:
              for c in range(NT):
                  dbg_dump("h0", h[:, c, :], (slice(c * 128, (c + 1) * 128), slice(None)), [B_h[c]])

          for l in range(depth):
              P.barrier()
              checkpoint(1)
              cv = Carver()
              stage = [cv.take([128, NT, 512], BF16) for _ in range(2)]
              B_stage = [Buf("stage0"), Buf("stage1")]
              stageT = [cv.take([128, 4, T], BF16) for _ in range(2)]
              B_stageT = [Buf("stageT0"), Buf("stageT1")]
              stageV = [cv.take([128, 4, NT, 129], BF16) for _ in range(2)]
              B_stageV = [Buf("stageV0"), Buf("stageV1")]
              rA = [cv.take([128, 512], F32)] * 2
              rB = [cv.take([128, 512], F32)] * 2
              B_r = [Buf("r0")] * 2
              rk_cnt = [0]

              def rope_epi(fam, ub, c, ps, bps, k):
                  if fam in ("aq", "ak"):
                      half, nb, co, so = 32, 8, 0, 32
                  else:
                      half, nb, co, so = 64, 4, 64, 128
                  j = rk_cnt[0] % 2
                  rk_cnt[0] += 1
                  psv = ps[:].rearrange("p (b two x) -> p b two x", two=2, x=half)
                  Av = rA[j].rearrange("p (b two x) -> p b two x", two=2, x=half)
                  Bv = rB[j].rearrange("p (b two x) -> p b two x", two=2, x=half)
                  cosb = rope_t[:, c, co:co + half].unsqueeze(1).unsqueeze(1).broadcast_to([128, nb, 2, half])
                  sinb = rope_t[:, c, so:so + half].unsqueeze(1).broadcast_to([128, nb, half])
                  dve.op(mk("tensor_tensor", out=Av, in0=psv, in1=cosb, op=ALU.mult), reads=[bps, B_rope], writes=[B_r[j]])
                  dve.op(mk("tensor_tensor", out=Bv[:, :, 0, :], in0=psv[:, :, 1, :], in1=sinb, op=ALU.mult), reads=[bps, B_rope], writes=[B_r[j]])
                  dve.op(mk("tensor_tensor", out=Bv[:, :, 1, :], in0=psv[:, :, 0, :], in1=sinb, op=ALU.mult), reads=[bps, B_rope], writes=[B_r[j]])
                  if fam in ("aq", "ak"):
                      sv = stage[k][:, c, :].rearrange("p (b two x) -> p b two x", two=2, x=half)
                      dve.op(mk("tensor_tensor", out=sv[:, :, 0, :], in0=Av[:, :, 0, :], in1=Bv[:, :, 0, :], op=ALU.subtract), reads=[B_r[j]], writes=[B_stage[k]])
                      dve.op(mk("tensor_tensor", out=sv[:, :, 1, :], in0=Av[:, :, 1, :], in1=Bv[:, :, 1, :], op=ALU.add), reads=[B_r[j]], writes=[B_stage[k]])
                  else:
                      dve.op(mk("tensor_tensor", out=Av[:, :, 0, :], in0=Av[:, :, 0, :], in1=Bv[:, :, 0, :], op=ALU.subtract), reads=[B_r[j]], writes=[B_r[j]])
                      dve.op(mk("tensor_tensor", out=Av[:, :, 1, :], in0=Av[:, :, 1, :], in1=Bv[:, :, 1, :], op=ALU.add), reads=[B_r[j]], writes=[B_r[j]])
                      if fam == "rq":
                          base = CST_QDEC
                      else:
                          base = CST_KINV0 if (c0 + c) == 0 else CST_KINV
                      scl = cst[:, base + ub * 4: base + ub * 4 + 4].unsqueeze(2).broadcast_to([128, 4, 128])
                      a3 = rA[j].rearrange("p (b x) -> p b x", x=128)
                      s3 = stage[k][:, c, :].rearrange("p (b x) -> p b x", x=128)
                      dve.op(mk("tensor_tensor", out=s3, in0=a3, in1=scl, op=ALU.mult), reads=[B_r[j], B_const], writes=[B_stage[k]])

              ucount = [0]

              def run_unit(fam, ub):
                  k = ucount[0] % 2
                  ucount[0] += 1
                  uinfo, W, B_W = take_unit()

                  if fam in ("aq", "ak", "rq", "rk"):
                      def epi(c, ps, bps):
                          rope_epi(fam, ub, c, ps, bps, k)
                          return lambda c=c: transpose_cols(stage[k][:, c, :], B_stage[k], 4, lambda j0, n, c=c: stageT[k][:, j0:j0 + n, c * 128:(c + 1) * 128], B_stageT[k], [act])
                  elif fam == "av":
                      def epi(c, ps, bps):
                          act.op(mk("copy", out=stageV[k][:, :, c, 0:128], in_=ps[:].rearrange("p (a x) -> p a x", x=128)), reads=[bps], writes=[B_stageV[k]])
                          if c == 0:
                              pool.op(mk("memset", stageV[k][:, :, :, 128:129], 1.0), writes=[B_stageV[k]])
                          if (c0 + c) == 0:
                              pool.op(mk("tensor_scalar", out=stageV[k][:, :, 0, :], in0=stageV[k][:, :, 0, :], scalar1=cst[:, CST_VALID:CST_VALID + 1], scalar2=None, op0=ALU.mult),
                                      reads=[B_const, B_stageV[k]], writes=[B_stageV[k]])
                  elif fam == "rv":
                      def epi(c, ps, bps):
                          act.op(mk("copy", out=stage[k][:, c, :], in_=ps[:]), reads=[bps], writes=[B_stage[k]])
                  elif fam == "rg":
                      def epi(c, ps, bps):
                          act.op(mk("activation", out=stage[k][:, c, :], in_=ps[:], func=AF.Silu), reads=[bps], writes=[B_stage[k]])
                  else:
                      def epi(c, ps, bps):
                          act.op(mk("activation", out=stage[k][:, c, :], in_=ps[:], func=AF.Sigmoid), reads=[bps], writes=[B_stage[k]])
                  gemm_tok(AT, B_A, T, NT, W, B_W, 512, epi)
                  if fam == "aq":
                      sp.op(mk("dma_start", out=qT_s[ub * 4:ub * 4 + 4, :, 0:T].rearrange("a p t -> p a t"), in_=stageT[k]), reads=[B_stageT[k]], writes=[B_qT])
                  elif fam == "ak":
                      sp.op(mk("dma_start", out=kT_c[l, ub * 4:ub * 4 + 4, :, c0 * 128:c0 * 128 + T].rearrange("a p t -> p a t"), in_=stageT[k]), reads=[B_stageT[k]], writes=[B_kT])
                  elif fam == "rq":
                      sp.op(mk("dma_start", out=qdT_s[ub * 4:ub * 4 + 4, :, 0:T].rearrange("a p t -> p a t"), in_=stageT[k]), reads=[B_stageT[k]], writes=[B_qdT])
                  elif fam == "rk":
                      sp.op(mk("dma_start", out=kiT_s[ub * 4:ub * 4 + 4, :, 0:T].rearrange("a p t -> p a t"), in_=stageT[k]), reads=[B_stageT[k]], writes=[B_kiT])
                      sp.op(mk("dma_start", out=ki_s[0:T, ub * 512:(ub + 1) * 512].rearrange("(c p) f -> p c f", p=128), in_=stage[k]), reads=[B_stage[k]], writes=[B_ki])
                  elif fam == "av":
                      sp.op(mk("dma_start", out=v_c[l, ub * 4:ub * 4 + 4, :, c0 * 129:(c0 + NT) * 129].rearrange("a p x -> p a x"),
                                                  in_=stageV[k].rearrange("p a c x -> p a (c x)")), reads=[B_stageV[k]], writes=[B_v])
                  else:
                      dst = {"rv": (rv_s, B_rv), "rg": (sg_s, B_sg), "ga": (sga_s, B_sga), "gb": (sgb_s, B_sgb)}[fam]
                      sp.op(mk("dma_start", out=dst[0][0:T, ub * 512:(ub + 1) * 512].rearrange("(c p) f -> p c f", p=128), in_=stage[k]), reads=[B_stage[k]], writes=[dst[1]])

              fams = [("aq", 4), ("ak", 4), ("av", 4), ("rq", 2), ("rk", 2), ("rv", 4), ("rg", 4), ("ga", 4), ("gb", 4)]
              for fam, n in fams:
                  for ub in range(n):
                      run_unit(fam, ub)

              P.barrier()
              checkpoint(2)
              cv = Carver()
              qTh0 = [cv.take([128, T], BF16) for _ in range(2)]
              qTh1 = [cv.take([128, T], BF16) for _ in range(2)]
              kTh = [cv.take([128, Lk], BF16) for _ in range(2)]
              Vh = [cv.take([128, nkc_tile, 129], BF16) for _ in range(2)]
              B_ld = [Buf("attld0"), Buf("attld1")]
              for k_ in range(2):
                  pool.op(mk("memset", qTh0[k_][64:128, :], 0.0), writes=[B_ld[k_]])
                  pool.op(mk("memset", qTh1[k_][0:64, :], 0.0), writes=[B_ld[k_]])
              NE = 4
              Et = [cv.take([128, 512], BF16) for _ in range(NE)]
              B_E = [Buf(f"E{i}") for i in range(NE)]
              junk = cv.take([128, 128], F32)
              ecnt = [0]
              ocnt = [0]
              ps_groups["acc"] = [0, 1]
              ps_groups["mm"] = [2, 3, 4, 5, 6]
              ps_groups["tp"] = [7]
              ps_rr["acc"] = 0
              ps_rr["mm"] = 0
              items = []
              for a in range(16):
                  for i in range(NT):
                      gi = c0 + i
                      for m in range(2):
                          g0s = list(range(0, gi + 1, 4))
                          for g0 in g0s:
                              items.append((a, i, m, g0, g0 == g0s[-1]))
              loaded = set()
              Ocur = {}

              def emit_S(it):
                  a, i, m, g0, last = it
                  k = a % 2
                  gi = c0 + i
                  if a not in loaded:
                      loaded.add(a)
                      sp.op(mk("dma_start", out=qTh0[k][0:64, :], in_=qT_s[a, 0:64, 0:T]), reads=[B_qT], writes=[B_ld[k]])
                      sp.op(mk("dma_start", out=qTh1[k][64:128, :], in_=qT_s[a, 64:128, 0:T]), reads=[B_qT], writes=[B_ld[k]])
                      sp.op(mk("dma_start", out=kTh[k], in_=kT_c[l, a, :, 0:Lk]), reads=[B_kT], writes=[B_ld[k]])
                      sp.op(mk("dma_start", out=Vh[k].rearrange("p c x -> p (c x)"), in_=v_c[l, a, :, 0:nkc_tile * 129]), reads=[B_v], writes=[B_ld[k]])
                  pr = slice(m * 64, (m + 1) * 64)
                  js = list(range(g0, min(g0 + 4, gi + 1)))
                  S, bS = next_ps("mm")
                  for jj, j in enumerate(js):
                      pe.op(mk("matmul", S[:, jj * 128:(jj + 1) * 128], lhsT=kTh[k][:, j * 128:(j + 1) * 128],
                               rhs=(qTh0 if m == 0 else qTh1)[k][:, i * 128:(i + 1) * 128], start=True, stop=True),
                            reads=[B_ld[k]], writes=[bS], signal=(jj == len(js) - 1))
                  return (S, bS, js)

              def emit_rest(it, st):
                  a, i, m, g0, last = it
                  S, bS, js = st
                  k = a % 2
                  gi = c0 + i
                  if g0 == 0:
                      Ocur[m] = next_ps("acc")
                  O, bO = Ocur[m]
                  ei = ecnt[0] % NE
                  ecnt[0] += 1
                  E, bE = Et[ei], B_E[ei]
                  n = len(js)
                  act.op(mk("activation", out=E[:, 0:n * 128], in_=S[:, 0:n * 128], func=AF.Exp, scale=0.125), reads=[bS], writes=[bE])
                  if gi in js:
                      jj = js.index(gi)
                      dve.op(mk("tensor_tensor", out=E[:, jj * 128:(jj + 1) * 128], in0=E[:, jj * 128:(jj + 1) * 128], in1=tri_bf[:], op=ALU.mult),
                             reads=[bE, B_const], writes=[bE])
                  for jj, j in enumerate(js):
                      pe.op(mk("matmul", O[:, 0:129], lhsT=E[:, jj * 128:(jj + 1) * 128], rhs=Vh[k][:, j, :],
                               start=(j == 0), stop=(j == gi)),
                            reads=[bE, B_ld[k]], writes=[bO], signal=(j == gi))
                  if not (last and m == 1):
                      return
                  oi = ocnt[0] % NOR
                  ocnt[0] += 1
                  (O0, bO0), (O1, bO1) = Ocur[0], Ocur[1]
                  s_ = sc[oi]
                  o_ = oA[oi]
                  Bo = B_o[oi]
                  dve.op(mk("tensor_scalar", out=s_[:, 0:1], in0=O0[:, 128:129], scalar1=cst[:, CST_TINY:CST_TINY + 1], scalar2=None, op0=ALU.max), reads=[bO0, B_const], writes=[Bo])
                  dve.op(mk("tensor_scalar", out=s_[:, 1:2], in0=O1[:, 128:129], scalar1=cst[:, CST_TINY:CST_TINY + 1], scalar2=None, op0=ALU.max), reads=[bO1, B_const], writes=[Bo])
                  dve.op(mk("reciprocal", out=s_[:, 2:4], in_=s_[:, 0:2]), reads=[Bo], writes=[Bo])
                  dve.op(mk("tensor_tensor", out=s_[:, 4:5], in0=s_[:, 3:4], in1=lam_t[:, l, 0:1], op=ALU.mult), reads=[Bo, B_lay], writes=[Bo])
                  dve.op(mk("tensor_scalar", out=o_, in0=O0[:, 0:128], scalar1=s_[:, 2:3], scalar2=None, op0=ALU.mult), reads=[bO0, Bo], writes=[Bo])
                  dve.op(mk("scalar_tensor_tensor", out=o_, in0=O1[:, 0:128], scalar=s_[:, 4:5], in1=o_, op0=ALU.mult, op1=ALU.add), reads=[bO1, Bo], writes=[Bo])
                  dve.op(mk("scalar_tensor_tensor", out=junk, in0=o_, scalar=1.0, in1=o_, op0=ALU.mult, op1=ALU.mult, accum_out=s_[:, 5:6]), reads=[Bo], writes=[Bo, B_junk])
                  y_ = Yb[oi]

                  def part2(s_=s_, o_=o_, y_=y_, Bo=Bo):
                      pool.op(mk("tensor_scalar", out=s_[:, 6:7], in0=s_[:, 5:6], scalar1=1.0 / 128.0, scalar2=EPS, op0=ALU.mult, op1=ALU.add), reads=[Bo], writes=[Bo])
                      pool.op(mk("tensor_tensor", out=s_[:, 7:8], in0=s_[:, 6:7], in1=cst[:, CST_NHALF:CST_NHALF + 1], op=ALU.pow), reads=[Bo, B_const], writes=[Bo])
                      dve.op(mk("scalar_tensor_tensor", out=y_, in0=o_, scalar=s_[:, 7:8], in1=subg_bc[:, l, :], op0=ALU.mult, op1=ALU.mult), reads=[Bo, B_lay], writes=[Bo])

                  def part3(y_=y_, Bo=Bo, a=a, i=i):
                      transpose_cols(y_, Bo, 1, lambda j0, n, a=a, i=i: BT[:, a:a + 1, i * 128:(i + 1) * 128], B_B, [dve])
                  etails.append([2, part2])
                  etails.append([4, part3])

              def run_tails(force=False):
                  keep = []
                  for t in etails:
                      t[0] -= 1
                  for t in list(etails):
                      if force or t[0] <= 0:
                          t[1]()
                          etails.remove(t)

              etails = []
              NOR = 6
              oA = [cv.take([128, 128], F32) for _ in range(NOR)]
              Yb = [cv.take([128, 128], BF16) for _ in range(NOR)]
              sc = [cv.take([128, 8], F32) for _ in range(NOR)]
              B_o = [Buf(f"o{x}") for x in range(NOR)]
              B_junk = Buf("junk")
              DEPTH_PIPE = 4
              pipe = []
              for it in items:
                  pipe.append((it, emit_S(it)))
                  if len(pipe) > DEPTH_PIPE:
                      emit_rest(*pipe.pop(0))
                      run_tails()
              while pipe:
                  emit_rest(*pipe.pop(0))
                  run_tails()
              while etails:
                  run_tails(force=True)
              ps_groups["acc"] = [0, 1, 2, 3]
              ps_groups["mm"] = [4, 5]
              ps_groups["tp"] = [6, 7]
              if "oaT" in dbg and ti == 0 and l == 0:
                  P.barrier()
                  tmpf = cv.take([128, T], F32)
                  for kc in range(16):
                      Bt = Buf("t")
                      dve.op(mk("tensor_copy", out=tmpf, in_=BT[:, kc, :]), reads=[B_B], writes=[Bt])
                      sp.op(mk("dma_start", out=dbg["oaT"][kc * 128:(kc + 1) * 128, 0:T], in_=tmpf), reads=[Bt])
                      P.barrier()

              P.barrier()
              checkpoint(3)
              cv = Carver()
              gst = [cv.take([128, NT, 512], BF16) for _ in range(2)]
              B_gst = [Buf("gst0"), Buf("gst1")]
              tmpf = [cv.take([128, 512], F32) for _ in range(2)]
              B_tmpf = [Buf("tmpf0"), Buf("tmpf1")]
              for u in range(4):
                  k = u % 2
                  uinfo, W, B_W = take_unit()
                  sp.op(mk("dma_start", out=gst[k], in_=sga_s[0:T, u * 512:(u + 1) * 512].rearrange("(c p) f -> p c f", p=128)), reads=[B_sga], writes=[B_gst[k]])

                  def epi(c, ps, bps, u=u, k=k):
                      dve.op(mk("tensor_tensor", out=A_tok[:, c, u * 512:(u + 1) * 512], in0=ps[:], in1=gst[k][:, c, :], op=ALU.mult), reads=[bps, B_gst[k]], writes=[B_A])
                  gemm_tok(BT, B_B, T, NT, W, B_W, 512, epi)

              P.barrier()
              checkpoint(4)
              cv = Carver()
              retg_bc = cv.take([128, D], F32)
              sp.op(mk("dma_start", out=retg_bc, in_=retg[l:l + 1, :].partition_broadcast(128)[:, 0, :]), writes=[B_lay])
              qdh = [cv.take([128, T], BF16) for _ in range(2)]
              kih = [cv.take([128, T], BF16) for _ in range(2)]
              kth = [cv.take([128, NT, 128], BF16) for _ in range(2)]
              rvh = [cv.take([128, NT, 256], BF16) for _ in range(2)]
              sgh = [cv.take([128, NT, 256], BF16) for _ in range(2)]
              B_rl = [Buf("rl0"), Buf("rl1")]
              Sbf2 = [cv.take([128, 256], BF16) for _ in range(2)]
              B_Sbf2 = [Buf("Sbf0"), Buf("Sbf1")]
              B_st2 = [Buf("st0"), Buf("st1")]
              NRR = 4
              Pm = [cv.take([128, 128], BF16) for _ in range(NRR)]
              B_Pm = [Buf(f"Pm{x}") for x in range(NRR)]
              gs = [cv.take([128, 16], F32) for _ in range(NRR)]
              yf = [cv.take([128, 256], F32) for _ in range(NRR)]
              yb = [cv.take([128, 256], BF16) for _ in range(NRR)]
              B_y = [Buf(f"y{x}") for x in range(NRR)]
              rc = [0]
              rtails = []
              def ret_load(r):
                  k = r % 2
                  Sbf, B_Sbf = Sbf2[k], B_Sbf2[k]
                  sp.op(mk("dma_start", out=qdh[k], in_=qdT_s[r, :, 0:T]), reads=[B_qdT], writes=[B_rl[k]])
                  sp.op(mk("dma_start", out=kih[k], in_=kiT_s[r, :, 0:T]), reads=[B_kiT], writes=[B_rl[k]])
                  sp.op(mk("dma_start", out=kth[k], in_=ki_s[0:T, r * 128:(r + 1) * 128].rearrange("(c p) f -> p c f", p=128)), reads=[B_ki], writes=[B_rl[k]])
                  sp.op(mk("dma_start", out=rvh[k], in_=rv_s[0:T, r * 256:(r + 1) * 256].rearrange("(c p) f -> p c f", p=128)), reads=[B_rv], writes=[B_rl[k]])
                  sp.op(mk("dma_start", out=sgh[k], in_=sg_s[0:T, r * 256:(r + 1) * 256].rearrange("(c p) f -> p c f", p=128)), reads=[B_sg], writes=[B_rl[k]])
                  St = state[:, l, r, :]
                  dve.op(mk("tensor_copy", out=Sbf, in_=St), reads=[B_st2[k]], writes=[B_Sbf])

              def ret_iter(r, i):
                  k = r % 2
                  Sbf, B_Sbf = Sbf2[k], B_Sbf2[k]
                  St = state[:, l, r, :]
                  if True:
                      j = rc[0] % NRR
                      rc[0] += 1
                      cs = slice(i * 128, (i + 1) * 128)
                      S, bS = next_ps("mm")
                      pe.op(mk("matmul", S[:, 0:128], lhsT=kih[k][:, cs], rhs=qdh[k][:, cs], start=True, stop=True), reads=[B_rl[k]], writes=[bS])
                      dve.op(mk("tensor_tensor", out=Pm[j], in0=S[:, 0:128], in1=tri_bf[:], op=ALU.mult), reads=[bS, B_const], writes=[B_Pm[j]])
                      O, bO = next_ps()
                      pe.op(mk("matmul", O[:, 0:256], lhsT=Pm[j], rhs=rvh[k][:, i, :], start=True, stop=False), reads=[B_Pm[j], B_rl[k]], writes=[bO], signal=False)
                      pe.op(mk("matmul", O[:, 0:256], lhsT=qdh[k][:, cs], rhs=Sbf, start=False, stop=True), reads=[B_Sbf, B_rl[k]], writes=[bO])
                      U, bU = next_ps()
                      pe.op(mk("matmul", U[:, 0:256], lhsT=kth[k][:, i, :], rhs=rvh[k][:, i, :], start=True, stop=True), reads=[B_rl[k]], writes=[bU])
                      gcol = cst[:, CST_G128 + r:CST_G128 + r + 1]
                      dve.op(mk("tensor_scalar", out=St, in0=St, scalar1=gcol, scalar2=None, op0=ALU.mult), reads=[B_st2[k], B_const], writes=[B_st2[k]])
                      dve.op(mk("scalar_tensor_tensor", out=St, in0=U[:, 0:256], scalar=gcol, in1=St, op0=ALU.mult, op1=ALU.add), reads=[bU, B_st2[k], B_const], writes=[B_st2[k]])
                      dve.op(mk("tensor_copy", out=Sbf, in_=St), reads=[B_st2[k]], writes=[B_Sbf])
                      g_s = gs[j]
                      dve.op(mk("bn_stats", out=g_s[:, 0:6], in_=O[:, 0:256]), reads=[bO], writes=[B_y[j]])
                      dve.op(mk("bn_aggr", out=g_s[:, 6:8], in_=g_s[:, 0:6]), reads=[B_y[j]], writes=[B_y[j]])
                      act.op(mk("activation", out=g_s[:, 8:9], in_=g_s[:, 7:8], func=AF.Sqrt, bias=cst[:, CST_EPS:CST_EPS + 1], scale=1.0), reads=[B_y[j], B_const], writes=[B_y[j]])
                      dve.op(mk("reciprocal", out=g_s[:, 9:10], in_=g_s[:, 8:9]), reads=[B_y[j]], writes=[B_y[j]])
                      dve.op(mk("scalar_tensor_tensor", out=yf[j], in0=O[:, 0:256], scalar=g_s[:, 6:7], in1=retg_bc[:, r * 256:(r + 1) * 256], op0=ALU.subtract, op1=ALU.mult), reads=[bO, B_y[j], B_lay], writes=[B_y[j]])
                      dve.op(mk("scalar_tensor_tensor", out=yb[j], in0=yf[j], scalar=g_s[:, 9:10], in1=sgh[k][:, i, :], op0=ALU.mult, op1=ALU.mult), reads=[B_y[j], B_rl[k]], writes=[B_y[j]])
                      rtails.append(lambda j=j, r=r, i=i: transpose_cols(yb[j], B_y[j], 2, lambda j0, n, r=r, i=i: BT[:, 2 * r + j0:2 * r + j0 + n, i * 128:(i + 1) * 128], B_B, [act]))
                      if len(rtails) > 2:
                          rtails.pop(0)()

              for rp in range(0, 8, 2):
                  ret_load(rp)
                  ret_load(rp + 1)
                  for i in range(NT):
                      ret_iter(rp, i)
                      ret_iter(rp + 1, i)
              while rtails:
                  rtails.pop(0)()

              P.barrier()
              checkpoint(5)
              cv = Carver()
              gst = [cv.take([128, NT, 512], BF16) for _ in range(2)]
              B_gst = [Buf("gst0"), Buf("gst1")]
              tmpf = [cv.take([128, 512], F32) for _ in range(2)]
              B_tmpf = [Buf("tmpf0"), Buf("tmpf1")]
              tc_ = [0]
              for u in range(4):
                  k = u % 2
                  uinfo, W, B_W = take_unit()
                  sp.op(mk("dma_start", out=gst[k], in_=sgb_s[0:T, u * 512:(u + 1) * 512].rearrange("(c p) f -> p c f", p=128)), reads=[B_sgb], writes=[B_gst[k]])

                  def epi(c, ps, bps, u=u, k=k):
                      j = tc_[0] % 2
                      tc_[0] += 1
                      dve.op(mk("tensor_tensor", out=tmpf[j], in0=ps[:], in1=gst[k][:, c, :], op=ALU.mult), reads=[bps, B_gst[k]], writes=[B_tmpf[j]])
                      pool.op(mk("tensor_tensor", out=A_tok[:, c, u * 512:(u + 1) * 512], in0=tmpf[j], in1=A_tok[:, c, u * 512:(u + 1) * 512], op=ALU.add), reads=[B_tmpf[j], B_A], writes=[B_A])
                  gemm_tok(BT, B_B, T, NT, W, B_W, 512, epi)
              P.barrier()
              checkpoint(6)
              for c in range(NT):
                  transpose_cols(A_tok[:, c, :], B_A, 16, lambda j0, n, c=c: BT[:, j0:j0 + n, c * 128:(c + 1) * 128], B_B, [dve, act])

              P.barrier()
              checkpoint(7)
              cv = Carver()
              g_, b_, Bp = load_ln_params(cv, 2 + 4 * l, 3 + 4 * l)
              stats = cv.take([128, 4, 6], F32)
              mv = cv.take([128, 8], F32)
              Bs = Buf("lnstat")
              for u in range(4):
                  uinfo, W, B_W = take_unit()

                  def epi(c, ps, bps, u=u):
                      hs = h[:, c, u * 512:(u + 1) * 512]
                      dve.op(mk("scalar_tensor_tensor", out=hs, in0=hs, scalar=alpha, in1=ps[:], op0=ALU.mult, op1=ALU.add), reads=[bps, B_h[c]], writes=[B_h[c]])
                  gemm_tok(BT, B_B, T, NT, W, B_W, 512, epi)
              for c in range(NT):
                  layer_norm_chunk(c, g_, b_, Bp, stats, mv, Bs)
              make_T_from_h(cv, NT, T, AT, B_A)
              if ti == 0 and l == 0:
                  for c in range(NT):
                      dbg_dump("h1", h[:, c, :], (slice(c * 128, (c + 1) * 128), slice(None)), [B_h[c]])
              for c in range(NT):
                  act.op(mk("activation", out=h[:, c, :], in_=h[:, c, :], func=AF.Copy, scale=alpha), reads=[B_h[c]], writes=[B_h[c]])

              P.barrier()
              checkpoint(8)
              cv = Carver()
              nblk = (T + 511) // 512
              bs = T // nblk
              assert bs * nblk == T
              actT = [cv.take([128, 4, T], BF16) for _ in range(2)]
              B_actT = [Buf("actT0"), Buf("actT1")]
              gl = cv.take([128, 4, T], F32)
              B_gl = Buf("gl")
              ug = [cv.take([128, T + 2], F32) for _ in range(2)]
              B_u = [Buf("u0"), Buf("u1")]
              t1 = [cv.take([128, T], F32) for _ in range(2)]
              B_t1 = [Buf("t10"), Buf("t11")]
              uc = [0]

              def emit_wdown(g):
                  nf = min(4, NFC - 4 * g)
                  ka = g % 2
                  _, Wd, B_Wd = take_unit()
                  Wdv = Wd[:, 0:nf * 2048].rearrange("p (f c) -> p f c", c=2048)
                  for c in range(NT):
                      pump(1)
                      for n in range(4):
                          ps, bps = next_ps("mm")
                          for f in range(nf):
                              pe.op(mk("matmul", ps[:], lhsT=actT[ka][:, f, c * 128:(c + 1) * 128], rhs=Wdv[:, f, n * 512:(n + 1) * 512],
                                       start=(f == 0), stop=(f == nf - 1)),
                                    reads=[B_actT[ka], B_Wd[f]], writes=[bps], signal=(f == nf - 1))
                          hs = h[:, c, n * 512:(n + 1) * 512]
                          dve.op(mk("tensor_tensor", out=hs, in0=hs, in1=ps[:], op=ALU.add), reads=[bps, B_h[c]], writes=[B_h[c]])

              for g in range(ngrp):
                  nf = min(4, NFC - 4 * g)
                  ka = g % 2
                  ncol = nf * 128
                  for w in range(2):
                      _, Wt, B_Wt = take_unit()
                      Wv_ = Wt[:, 0:16 * ncol].rearrange("p (k c) -> p k c", c=ncol)
                      for f in range(nf):
                          fc = g * 4 + f
                          ci = fc + w * NFC
                          j = uc[0] % 2
                          uc[0] += 1
                          for tb in range(nblk):
                              ps, bps = next_ps()
                              for kc in range(16):
                                  pe.op(mk("matmul", ps[:, 0:bs], lhsT=Wv_[:, kc, f * 128:(f + 1) * 128], rhs=AT[:, kc, tb * bs:(tb + 1) * bs],
                                           start=(kc == 0), stop=(kc == 15)),
                                        reads=[B_A, B_Wt[kc // 4]], writes=[bps], signal=(kc == 15))
                              act.op(mk("copy", out=ug[j][:, 2 + tb * bs:2 + (tb + 1) * bs], in_=ps[:, 0:bs]), reads=[bps], writes=[B_u[j]])
                          pump(1)
                          pool.op(mk("tensor_copy", out=ug[j][:, 0:2], in_=halo[:, l, ci, :]), reads=[B_halo], writes=[B_u[j]])
                          if ti == 0:
                              pool.op(mk("memset", ug[j][:, 2:2 + PAD], 0.0), writes=[B_u[j]])
                          pool.op(mk("tensor_copy", out=halo[:, l, ci, :], in_=ug[j][:, T:T + 2]), reads=[B_u[j]], writes=[B_halo])
                          dve.op(mk("tensor_scalar", out=t1[j], in0=ug[j][:, 0:T], scalar1=cw[:, l, 0, ci:ci + 1], scalar2=cw[:, l, 3, ci:ci + 1], op0=ALU.mult, op1=ALU.add),
                                 reads=[B_u[j], B_lay], writes=[B_t1[j]])
                          dve.op(mk("scalar_tensor_tensor", out=t1[j], in0=ug[j][:, 1:T + 1], scalar=cw[:, l, 1, ci:ci + 1], in1=t1[j], op0=ALU.mult, op1=ALU.add),
                                 reads=[B_u[j], B_lay, B_t1[j]], writes=[B_t1[j]])
                          dve.op(mk("scalar_tensor_tensor", out=t1[j], in0=ug[j][:, 2:T + 2], scalar=cw[:, l, 2, ci:ci + 1], in1=t1[j], op0=ALU.mult, op1=ALU.add),
                                 reads=[B_u[j], B_lay, B_t1[j]], writes=[B_t1[j]])
                          if w == 0:
                              act.op(mk("activation", out=gl[:, f, :], in_=t1[j], func=AF.Gelu), reads=[B_t1[j]], writes=[B_gl])
                          else:
                              pool.op(mk("tensor_tensor", out=actT[ka][:, f, :], in0=gl[:, f, :], in1=t1[j], op=ALU.mult), reads=[B_gl, B_t1[j]], writes=[B_actT[ka]])
                  if g > 0:
                      emit_wdown(g - 1)
              emit_wdown(ngrp - 1)

              P.barrier()
              checkpoint(9)
              if ti == 0 and l == 0:
                  for c in range(NT):
                      dbg_dump("h2p", h[:, c, :], (slice(c * 128, (c + 1) * 128), slice(None)), [B_h[c]])
              cv = Carver()
              g_, b_, Bp = load_ln_params(cv, 4 + 4 * l, 5 + 4 * l)
              stats = cv.take([128, 4, 6], F32)
              mv = cv.take([128, 8], F32)
              Bs = Buf("lnstat")
              for c in range(NT):
                  layer_norm_chunk(c, g_, b_, Bp, stats, mv, Bs)
              if l + 1 < depth:
                  make_T_from_h(cv, NT, T, AT, B_A)
              if ti == 0 and l == 0:
                  for c in range(NT):
                      dbg_dump("h2", h[:, c, :], (slice(c * 128, (c + 1) * 128), slice(None)), [B_h[c]])

          for c in range(NT):
              sp.op(mk("dma_start", out=out_d[(c0 + c) * 128:(c0 + c + 1) * 128, :], in_=h[:, c, :]), reads=[B_h[c]])
          c0 += NT


    try:
        main_loop()
        assert ucur[0] == len(units), (ucur[0], len(units))
    except StopBuild:
        pass
    P.finish()
    return nc


def prep_shared(inputs, depth):
    f = lambda a: np.ascontiguousarray(np.asarray(a, dtype=np.float32))
    lnp = np.concatenate([f(inputs["ln_emb_g"])[None], f(inputs["ln_emb_b"])[None]] +
                         [np.stack([f(inputs["ln1_g"])[l], f(inputs["ln1_b"])[l], f(inputs["ln2_g"])[l], f(inputs["ln2_b"])[l]]) for l in range(depth)], axis=0)
    lamv = np.stack([f(inputs["lam_q1"]), f(inputs["lam_k1"]), f(inputs["lam_q2"]), f(inputs["lam_k2"])], axis=1)
    convw = np.concatenate([f(inputs["conv_w"]), f(inputs["conv_b"])[:, None, :]], axis=1).reshape(depth, 4, 2 * NFC, 128)
    return {
        "w_in": f(inputs["w_in"]), "w_a": f(inputs["w_branch_a"]), "w_b": f(inputs["w_branch_b"]), "w_o": f(inputs["w_out"]),
        "w_up": f(inputs["w_up"]), "w_dn": f(inputs["w_down"]), "lnp": np.ascontiguousarray(lnp),
        "retg": f(inputs["ret_gn_g"]), "subg": f(inputs["subln_g"]), "lamv": np.ascontiguousarray(lamv), "convw": np.ascontiguousarray(convw),
    }


def run(inputs, tiles, depth, core_ids, debug=None, trace=False):
    x = np.asarray(inputs["x"], dtype=np.float32)
    B, S, _ = x.shape
    nch = (S + 128) // 128
    L = nch * 128
    nc = build_program(nch, tiles, depth, debug=debug)
    shared = prep_shared(inputs, depth)
    shared.update(make_consts(nch))
    meta = np.asarray(inputs["meta_tokens"], dtype=np.float32)
    in_maps = []
    for ci in core_ids:
        b = ci % B
        xin = np.zeros((L, D), np.float32)
        xin[PAD:PAD + N_META] = meta
        xin[128:] = x[b]
        m = dict(shared)
        m["xin"] = xin
        in_maps.append(m)
    res = run_bass_kernel_spmd(nc, in_maps, core_ids=list(core_ids), trace=trace)
    return res


def kernel(**inputs):
    depth = int(np.asarray(inputs["w_in"]).shape[0])
    x = np.asarray(inputs["x"])
    B, S, _ = x.shape
    nch = (S + 128) // 128
    tiles = []
    rem = nch
    while rem > 0:
        t = min(5 if not tiles else 4, rem)
        tiles.append(t)
        rem -= t
    res = run(inputs, tiles, depth, list(range(B)))
    out = np.stack([np.asarray(r["out"], dtype=np.float32)[128:] for r in res.results[:B]], axis=0)
    return out
```

```python
import math
import numpy as np
import ml_dtypes
import concourse.bass as bass
import concourse.mybir as mybir
from concourse.bass_utils import run_bass_kernel_spmd

F32 = mybir.dt.float32
BF16 = mybir.dt.bfloat16
AF = mybir.ActivationFunctionType
ALU = mybir.AluOpType

D = 2048
DFF = 5504
NFC = 43
EPS = 1e-5
N_META = 16
PAD = 112
ROPE_THETA = 10000.0


def mk(name, *args, **kw):
    return lambda e: getattr(e, name)(*args, **kw)


class Buf:
    __slots__ = ("name", "lw", "rd")

    def __init__(self, name):
        self.name = name
        self.lw = None
        self.rd = []


class Eng:
    def __init__(self, prog, name, sems, is_dma=False):
        self.prog = prog
        self.name = name
        self.sems = sems
        self.is_dma = is_dma
        self.ops = []
        self.cnt = 0
        self.waited = {}
        self.pend_r = []
        self.pend_w = []

    def _wait(self, ev):
        if ev is None:
            return
        si, val = ev
        if self.waited.get(si, 0) >= val:
            return
        self.waited[si] = val
        sem = self.prog.semlist[si]
        self.ops.append(mk("wait_ge", sem, val))

    def _deps(self, reads, writes):
        for b in reads:
            self._wait(b.lw)
        for b in writes:
            self._wait(b.lw)
            for ev in b.rd:
                self._wait(ev)

    def op(self, fn, reads=(), writes=(), signal=True, after=()):
        self._deps(reads, writes)
        for b in after:
            for ev in b.rd:
                self._wait(ev)
        if self.is_dma:
            K = len(self.sems)
            n = self.cnt
            self.cnt += 1
            slot = n % K
            need = 16 * (n // K)
            si = self.sems[slot]
            self._wait((si, need))
            ev = (si, need + 16)
            sem = self.prog.semlist[si]
            self.ops.append(lambda e, fn=fn, sem=sem: fn(e).then_inc(sem, 16))
        elif signal:
            self.cnt += 1
            si = self.sems[0]
            ev = (si, self.cnt)
            sem = self.prog.semlist[si]
            self.ops.append(lambda e, fn=fn, sem=sem: fn(e).then_inc(sem, 1))
        else:
            self.ops.append(lambda e, fn=fn: fn(e))
            self.pend_r.extend(reads)
            self.pend_w.extend(writes)
            return None
        for b in list(reads) + self.pend_r:
            b.rd.append(ev)
        for b in list(writes) + self.pend_w:
            b.lw = ev
            b.rd = []
        self.pend_r = []
        self.pend_w = []
        return ev

    def last_event(self):
        if self.is_dma:
            evs = []
            K = len(self.sems)
            for slot in range(K):
                cnt = (self.cnt - slot + K - 1) // K if self.cnt > slot else 0
                if cnt > 0:
                    evs.append((self.sems[slot], 16 * cnt))
            return evs
        if self.cnt == 0:
            return []
        return [(self.sems[0], self.cnt)]


class Prog:
    def __init__(self, nc):
        self.nc = nc
        self.semlist = []

        def mk(n, name):
            ids = []
            for i in range(n):
                self.semlist.append(nc.alloc_semaphore(f"s_{name}{i}"))
                ids.append(len(self.semlist) - 1)
            return ids

        self.pe = Eng(self, "pe", mk(1, "pe"))
        self.act = Eng(self, "act", mk(1, "act"))
        self.dve = Eng(self, "dve", mk(1, "dve"))
        self.pool = Eng(self, "pool", mk(1, "pool"))
        self.sp = Eng(self, "sp", mk(12, "sp"), is_dma=True)
        self.engs = [self.pe, self.act, self.dve, self.pool, self.sp]

    def barrier(self):
        evs = []
        for e in self.engs:
            assert not e.pend_r and not e.pend_w, e.name
            evs.extend(e.last_event())
        for e in self.engs:
            for ev in evs:
                e._wait(ev)

    def finish(self):
        self.barrier()
        nc = self.nc
        with nc.Block() as block:
            @block.sync
            def _(e):
                for f in self.sp.ops:
                    f(e)

            @block.tensor
            def _(e):
                for f in self.pe.ops:
                    f(e)

            @block.scalar
            def _(e):
                for f in self.act.ops:
                    f(e)

            @block.vector
            def _(e):
                for f in self.dve.ops:
                    f(e)

            @block.gpsimd
            def _(e):
                for f in self.pool.ops:
                    f(e)


def make_consts(nch):
    L = nch * 128
    pos = (np.arange(L) - PAD).astype(np.float32)
    a_freq = (1.0 / (ROPE_THETA ** (np.arange(0, 64, 2, dtype=np.float32) / 64))).astype(np.float32)
    r_freq = (1.0 / (ROPE_THETA ** np.linspace(0.0, 1.0, 64, dtype=np.float32))).astype(np.float32)
    angA = (pos[:, None] * a_freq[None, :]).astype(np.float32)
    angR = (pos[:, None] * r_freq[None, :]).astype(np.float32)
    rope = np.concatenate([np.cos(angA), np.sin(angA), np.cos(angR), np.sin(angR)], axis=1).astype(np.float32)
    p = np.arange(128, dtype=np.float64)
    lg = np.log(1.0 - 2.0 ** (-5.0 - np.arange(8, dtype=np.float64)))
    qdec = np.exp(lg[None, :] * (p[:, None] + 1.0))
    kinv = np.exp(-lg[None, :] * (p[:, None] + 1.0)) * (128.0 ** -0.5)
    valid = (p >= PAD).astype(np.float64)[:, None]
    kinv0 = kinv * valid
    g128 = np.broadcast_to(np.exp(lg * 128.0)[None, :], (128, 8))
    cst = np.concatenate([qdec, kinv, kinv0, g128, valid, np.full((128, 1), EPS), np.full((128, 1), 1e-30), np.full((128, 1), -0.5)], axis=1).astype(np.float32)
    ident = np.eye(128, dtype=np.float32)
    tri = (np.arange(128)[:, None] <= np.arange(128)[None, :]).astype(np.float32)
    return {
        "rope": rope,
        "cst": cst,
        "ident_bf": ident.astype(ml_dtypes.bfloat16),
        "ident_f": ident,
        "tri_bf": tri.astype(ml_dtypes.bfloat16),
    }


CST_QDEC, CST_KINV, CST_KINV0, CST_G128, CST_VALID, CST_EPS, CST_TINY, CST_NHALF = 0, 8, 16, 24, 32, 33, 34, 35
CST_N = 36


def build_program(nch, tiles, depth, debug=None):
    assert sum(tiles) == nch
    L = nch * 128
    NTM = max(tiles)
    TM = NTM * 128
    alpha = float((2 * depth) ** 0.25)

    nc = bass.Bass("TRN2", target_bir_lowering=False)
    P = Prog(nc)
    import os
    STOP = int(os.environ.get("KSTOP", "99"))

    class StopBuild(Exception):
        pass

    def checkpoint(k):
        if STOP <= k:
            raise StopBuild()
    pe, act, dve, pool, sp = P.pe, P.act, P.dve, P.pool, P.sp

    def din(name, shape, dt=F32):
        return nc.dram_tensor(name, list(shape), dt, kind="ExternalInput").ap()

    def dscr(name, shape, dt=BF16):
        return nc.dram_tensor(name, list(shape), dt, kind="Internal").ap()

    xin = din("xin", [L, D])
    w_in = din("w_in", [depth, D, 16384])
    w_a = din("w_a", [depth, D, D])
    w_b = din("w_b", [depth, D, D])
    w_o = din("w_o", [depth, D, D])
    w_up = din("w_up", [depth, D, 2 * DFF])
    w_dn = din("w_dn", [depth, DFF, D])
    lnp = din("lnp", [2 + 4 * depth, D])
    retg = din("retg", [depth, D])
    subg = din("subg", [depth, 128])
    lamv = din("lamv", [depth, 4, 64])
    convw = din("convw", [depth, 4, 2 * NFC, 128])
    rope_d = din("rope", [L, 192])
    cst_d = din("cst", [128, CST_N])
    identbf_d = din("ident_bf", [128, 128], BF16)
    identf_d = din("ident_f", [128, 128])
    tribf_d = din("tri_bf", [128, 128], BF16)
    out_d = nc.dram_tensor("out", [L, D], F32, kind="ExternalOutput").ap()

    qT_s = dscr("qT_s", [16, 128, TM])
    kT_c = dscr("kT_c", [depth, 16, 128, L])
    v_c = dscr("v_c", [depth, 16, 128, nch * 129])
    qdT_s = dscr("qdT_s", [8, 128, TM])
    kiT_s = dscr("kiT_s", [8, 128, TM])
    ki_s = dscr("ki_s", [TM, 1024])
    rv_s = dscr("rv_s", [TM, D])
    sg_s = dscr("sg_s", [TM, D])
    sga_s = dscr("sga_s", [TM, D])
    sgb_s = dscr("sgb_s", [TM, D])
    UPL = 44 + 3 * ((NFC + 3) // 4)
    NUP = depth * UPL
    wcache_l = [dscr(f"wcache{l}", [UPL, 128, 8192]) for l in range(depth)]
    B_wc = [Buf(f"wc{i}") for i in range(NUP)]
    B_qT, B_kT, B_v = Buf("qT_s"), Buf("kT_c"), Buf("v_c")
    B_qdT, B_kiT, B_ki, B_rv, B_sg, B_sga, B_sgb = (Buf(n) for n in ["qdT", "kiT", "ki", "rv", "sg", "sga", "sgb"])

    dbg = {}
    if debug:
        for name, shape in debug.items():
            dbg[name] = nc.dram_tensor("dbg_" + name, list(shape), F32, kind="ExternalOutput").ap()

    def sb(name, shape, dt):
        return nc.alloc_sbuf_tensor("sb_" + name, list(shape), dt)

    h = sb("h", [128, NTM, D], F32)
    B_h = [Buf(f"h{c}") for c in range(NTM)]
    bufA = sb("bufA", [128, 16 * TM], BF16)
    bufB = sb("bufB", [128, 16 * TM], BF16)
    B_A, B_B = Buf("bufA"), Buf("bufB")
    NST = 4
    wst = [sb(f"wst{i}", [128, 2048], F32) for i in range(NST)]
    B_wst = [Buf(f"wst{i}") for i in range(NST)]
    wbf = [sb(f"wbf{i}", [128, 8192], BF16) for i in range(2)]
    B_wbf = [[Buf(f"wbf{i}_{q}") for q in range(4)] for i in range(2)]
    state = sb("state", [128, depth, 8, 256], F32)
    B_state = Buf("state")
    cst = sb("cst", [128, CST_N], F32)
    ident_bf = sb("ident_bf", [128, 128], BF16)
    ident_f = sb("ident_f", [128, 128], F32)
    tri_bf = sb("tri_bf", [128, 128], BF16)
    B_const = Buf("const")
    cw = sb("cw", [128, depth, 4, 2 * NFC], F32)
    B_cw = Buf("cw")
    halo = sb("halo", [128, depth, 2 * NFC, 2], F32)
    B_halo = Buf("halo")
    subg_bc = sb("subg_bc", [128, depth, 128], F32)
    lam_t = sb("lam_t", [128, depth, 8], F32)
    rope_t = sb("rope_t", [128, NTM, 192], F32)
    B_rope = Buf("rope")
    B_lay = Buf("layerparams")
    SCR_BYTES = 35 * 1024
    scr = sb("scr", [128, SCR_BYTES // 2], BF16)

    psb = [nc.alloc_psum_tensor(f"ps{i}", [128, 512], F32) for i in range(8)]
    B_ps = [Buf(f"ps{i}") for i in range(8)]
    ps_groups = {"acc": [0, 1, 2, 3], "mm": [4, 5], "tp": [6, 7]}
    ps_rr = {"acc": 0, "mm": 0, "tp": 0}

    def next_ps(grp="acc"):
        lst = ps_groups[grp]
        i = lst[ps_rr[grp] % len(lst)]
        ps_rr[grp] += 1
        return psb[i], B_ps[i]

    class Carver:
        def __init__(self):
            self.off = 0

        def take(self, shape, dt):
            n = int(np.prod(shape[1:]))
            nb = n * (4 if dt == F32 else 2)
            nb = (nb + 31) // 32 * 32
            assert self.off + nb <= SCR_BYTES, (self.off, nb, SCR_BYTES)
            ap = scr[:, self.off // 2:(self.off + nb) // 2]
            self.off += nb
            if dt == F32:
                ap = ap.bitcast(F32)
            ap = ap[:, 0:n]
            if len(shape) == 3:
                ap = ap.rearrange("p (a b) -> p a b", b=shape[2])
            elif len(shape) == 4:
                ap = ap.rearrange("p (a b c) -> p a b c", b=shape[2], c=shape[3])
            return ap

    sp.op(mk("dma_start", out=cst[:], in_=cst_d), writes=[B_const])
    sp.op(mk("dma_start", out=ident_bf[:], in_=identbf_d), writes=[B_const])
    sp.op(mk("dma_start", out=ident_f[:], in_=identf_d), writes=[B_const])
    sp.op(mk("dma_start", out=tri_bf[:], in_=tribf_d), writes=[B_const])
    pool.op(mk("memset", state[:], 0.0), writes=[B_state])
    pool.op(mk("memset", halo[:], 0.0), writes=[B_halo])
    for l in range(depth):
        sp.op(mk("dma_start", out=subg_bc[:, l, :], in_=subg[l:l + 1, :].partition_broadcast(128)[:, 0, :]), writes=[B_lay])
    P.barrier()
    cv = Carver()
    lq = cv.take([128, depth, 4, 64], F32)
    lj = cv.take([128, 64], F32)
    for l in range(depth):
        sp.op(mk("dma_start", out=lq[:, l, :, :], in_=lamv[l:l + 1, :, :].partition_broadcast(128)[:, 0, :, :]), writes=[B_lay])
    for l in range(depth):
        lam_init = 0.8 - 0.6 * math.exp(-0.3 * l)
        for k in range(2):
            dve.op(mk("tensor_tensor", out=lj, in0=lq[:, l, 2 * k, :], in1=lq[:, l, 2 * k + 1, :], op=ALU.mult), reads=[B_lay], writes=[B_lay])
            dve.op(mk("tensor_reduce", out=lam_t[:, l, 1 + k:2 + k], in_=lj, op=ALU.add, axis=mybir.AxisListType.X), reads=[B_lay], writes=[B_lay])
        act.op(mk("activation", out=lam_t[:, l, 3:5], in_=lam_t[:, l, 1:3], func=AF.Exp), reads=[B_lay], writes=[B_lay])
        dve.op(mk("tensor_tensor", out=lam_t[:, l, 5:6], in0=lam_t[:, l, 4:5], in1=lam_t[:, l, 3:4], op=ALU.subtract), reads=[B_lay], writes=[B_lay])
        dve.op(mk("tensor_scalar", out=lam_t[:, l, 0:1], in0=lam_t[:, l, 5:6], scalar1=-lam_init, scalar2=None, op0=ALU.add), reads=[B_lay], writes=[B_lay])
        dve.op(mk("tensor_scalar", out=subg_bc[:, l, :], in0=subg_bc[:, l, :], scalar1=1.0 - lam_init, scalar2=None, op0=ALU.mult), reads=[B_lay], writes=[B_lay])
    cwl = cv.take([128, depth * 4, 128], F32)
    for l in range(depth):
        for j in range(4):
            sp.op(mk("dma_start", out=cwl[0:2 * NFC, l * 4 + j, :], in_=convw[l, j, :, :]), writes=[B_cw])
    for l in range(depth):
        for j in range(4):
            ps, bps = next_ps("tp")
            pe.op(mk("transpose", out=ps[:, 0:2 * NFC], in_=cwl[0:2 * NFC, l * 4 + j, :], identity=ident_f[0:2 * NFC, 0:2 * NFC]),
                  reads=[B_cw, B_const], writes=[bps])
            dve.op(mk("tensor_copy", out=cw[:, l, j, :], in_=ps[:, 0:2 * NFC]), reads=[bps], writes=[B_lay])
    P.barrier()

    units = []
    unit_state = {"emitted": 0, "stq": 0}
    cast_eng_cycle = [act, dve, act, dve]
    pending_casts = []

    def emit_unit_load(ui):
        u = units[ui]
        slot = ui % 2
        cidx = ui % NUP
        if ui >= NUP:
            sp.op(mk("dma_start", out=wbf[slot][:, :], in_=wcache_l[cidx // UPL][cidx % UPL, :, :]), reads=[B_wc[cidx]], writes=B_wbf[slot])
            return
        nq = len(u["quarters"])
        for qi, (src, ne, shp) in enumerate(u["quarters"]):
            si = unit_state["stq"] % NST
            unit_state["stq"] += 1
            st = wst[si]
            if len(shp) == 2:
                dst = st[:, 0:ne].rearrange("p (a b) -> p a b", b=shp[1])
            else:
                dst = st[:, 0:ne]
            sp.op(mk("dma_start", out=dst, in_=src), writes=[B_wst[si]])
            ce = cast_eng_cycle[qi % 4]
            o = wbf[slot][:, qi * ne: qi * ne + ne]
            i_ = st[:, 0:ne]

            def do_cast(ce=ce, o=o, i_=i_, si=si, slot=slot, qi=qi, nq=nq, cidx=cidx):
                if ce is act:
                    ce.op(mk("copy", out=o, in_=i_), reads=[B_wst[si]], writes=[B_wbf[slot][qi]], after=B_wbf[slot])
                else:
                    ce.op(mk("tensor_copy", out=o, in_=i_), reads=[B_wst[si]], writes=[B_wbf[slot][qi]], after=B_wbf[slot])
                if qi == nq - 1 and len(tiles) > 1:
                    sp.op(mk("dma_start", out=wcache_l[cidx // UPL][cidx % UPL, :, :], in_=wbf[slot][:, :]), reads=B_wbf[slot], writes=[B_wc[cidx]])
            pending_casts.append((ui, do_cast))

    def pump(n=1):
        for _ in range(n):
            if pending_casts:
                pending_casts.pop(0)[1]()

    def get_unit(ui):
        def flush():
            while pending_casts and pending_casts[0][0] <= ui:
                pending_casts.pop(0)[1]()
        flush()
        while unit_state["emitted"] <= ui:
            emit_unit_load(unit_state["emitted"])
            unit_state["emitted"] += 1
            flush()
        if unit_state["emitted"] == ui + 1 and ui + 1 < len(units):
            emit_unit_load(ui + 1)
            unit_state["emitted"] += 1
        return wbf[ui % 2], B_wbf[ui % 2]

    def unit_cols(wmat, l, col0, ncol):
        qs = []
        for q in range(4):
            src = wmat[l, q * 512:(q + 1) * 512, col0:col0 + ncol].rearrange("(k p) c -> p k c", p=128)
            qs.append((src, 4 * ncol, (4, ncol)))
        return {"quarters": qs, "kind": "cols", "ncol": ncol}

    def unit_rows(wmat, l, row0, nf):
        qs = []
        for f in range(nf):
            src = wmat[l, row0 + f * 128: row0 + (f + 1) * 128, :]
            qs.append((src, 2048, (2048,)))
        return {"quarters": qs, "kind": "rows", "nf": nf}

    ngrp = (NFC + 3) // 4
    for ti in range(len(tiles)):
        for l in range(depth):
            for u in range(32):
                units.append(unit_cols(w_in, l, u * 512, 512))
            for u in range(4):
                units.append(unit_cols(w_a, l, u * 512, 512))
            for u in range(4):
                units.append(unit_cols(w_b, l, u * 512, 512))
            for u in range(4):
                units.append(unit_cols(w_o, l, u * 512, 512))
            for g in range(ngrp):
                nf = min(4, NFC - 4 * g)
                units.append(unit_cols(w_up, l, g * 512, nf * 128))
                units.append(unit_cols(w_up, l, DFF + g * 512, nf * 128))
                if g > 0:
                    units.append(unit_rows(w_dn, l, (g - 1) * 512, 4))
            units.append(unit_rows(w_dn, l, (ngrp - 1) * 512, NFC - 4 * (ngrp - 1)))
    ucur = [0]

    def take_unit():
        ui = ucur[0]
        ucur[0] += 1
        t, b = get_unit(ui)
        return units[ui], t, b

    def gemm_tok(AT, B_AT, T, NT, W, B_W, ncol, epi):
        Wv = W[:, 0:16 * ncol].rearrange("p (k c) -> p k c", c=ncol)
        tails = []
        for c in range(NT):
            ps, bps = next_ps()
            for kc in range(16):
                pe.op(mk("matmul", ps[:, 0:ncol], lhsT=AT[:, kc, c * 128:(c + 1) * 128], rhs=Wv[:, kc, :],
                                                            start=(kc == 0), stop=(kc == 15)),
                      reads=[B_AT, B_W[kc // 4]], writes=[bps], signal=(kc == 15))
            pump(1)
            tl = epi(c, ps, bps)
            if tl is not None:
                tails.append(tl)
            if len(tails) > 2:
                tails.pop(0)()
        while tails:
            tails.pop(0)()

    def transpose_cols(src, B_src, nblk, dst_fn, B_dst, evac_engs):
        j = 0
        k = 0
        while j < nblk:
            n = min(8, nblk - j)
            ps, bps = next_ps("tp")
            pv = ps[:].bitcast(BF16).rearrange("p (a b) -> p a b", b=128)
            for jj in range(n):
                pe.op(mk("transpose", out=pv[:, jj, :], in_=src[:, (j + jj) * 128:(j + jj + 1) * 128], identity=ident_bf[:]),
                      reads=[B_src, B_const], writes=[bps], signal=(jj == n - 1))
            ee = evac_engs[k % len(evac_engs)]
            k += 1
            d = dst_fn(j, n)
            if ee is act:
                ee.op(mk("copy", out=d, in_=pv[:, 0:n, :]), reads=[bps], writes=[B_dst])
            else:
                ee.op(mk("tensor_copy", out=d, in_=pv[:, 0:n, :]), reads=[bps], writes=[B_dst])
            j += n

    def load_ln_params(cvr, row_g, row_b):
        g = cvr.take([128, D], F32)
        b = cvr.take([128, D], F32)
        B = Buf("lnp")
        sp.op(mk("dma_start", out=g, in_=lnp[row_g:row_g + 1, :].partition_broadcast(128)[:, 0, :]), writes=[B])
        sp.op(mk("dma_start", out=b, in_=lnp[row_b:row_b + 1, :].partition_broadcast(128)[:, 0, :]), writes=[B])
        return g, b, B

    def layer_norm_chunk(c, g, b, Bp, stats, mv, Bs):
        hc = h[:, c, :]
        for s in range(4):
            dve.op(mk("bn_stats", out=stats[:, s, :], in_=h[:, c, s * 512:(s + 1) * 512]), reads=[B_h[c]], writes=[Bs])
        dve.op(mk("bn_aggr", out=mv[:, 0:2], in_=stats.rearrange("p a b -> p (a b)")), reads=[Bs], writes=[Bs])
        act.op(mk("activation", out=mv[:, 2:3], in_=mv[:, 1:2], func=AF.Sqrt, bias=cst[:, CST_EPS:CST_EPS + 1], scale=1.0), reads=[Bs, B_const], writes=[Bs])
        dve.op(mk("reciprocal", out=mv[:, 3:4], in_=mv[:, 2:3]), reads=[Bs], writes=[Bs])
        dve.op(mk("scalar_tensor_tensor", out=hc, in0=hc, scalar=mv[:, 0:1], in1=g, op0=ALU.subtract, op1=ALU.mult), reads=[Bs, Bp, B_h[c]], writes=[B_h[c]])
        dve.op(mk("scalar_tensor_tensor", out=hc, in0=hc, scalar=mv[:, 3:4], in1=b, op0=ALU.mult, op1=ALU.add), reads=[Bs, Bp, B_h[c]], writes=[B_h[c]])

    def layer_norm_tile(NT, g, b, Bp, cvr):
        st = [cvr.take([128, 4, 6], F32) for _ in range(NT)]
        mvs = [cvr.take([128, 8], F32) for _ in range(NT)]
        Bsc = [Buf(f"lnst{c}") for c in range(NT)]
        for c in range(NT):
            for s_ in range(4):
                dve.op(mk("bn_stats", out=st[c][:, s_, :], in_=h[:, c, s_ * 512:(s_ + 1) * 512]), reads=[B_h[c]], writes=[Bsc[c]])
            dve.op(mk("bn_aggr", out=mvs[c][:, 0:2], in_=st[c].rearrange("p a b -> p (a b)")), reads=[Bsc[c]], writes=[Bsc[c]])
        for c in range(NT):
            act.op(mk("activation", out=mvs[c][:, 2:3], in_=mvs[c][:, 1:2], func=AF.Sqrt, bias=cst[:, CST_EPS:CST_EPS + 1], scale=1.0), reads=[Bsc[c], B_const], writes=[Bsc[c]])
        for c in range(NT):
            hc = h[:, c, :]
            dve.op(mk("reciprocal", out=mvs[c][:, 3:4], in_=mvs[c][:, 2:3]), reads=[Bsc[c]], writes=[Bsc[c]])
            dve.op(mk("scalar_tensor_tensor", out=hc, in0=hc, scalar=mvs[c][:, 0:1], in1=g, op0=ALU.subtract, op1=ALU.mult), reads=[Bsc[c], Bp, B_h[c]], writes=[B_h[c]])
            dve.op(mk("scalar_tensor_tensor", out=hc, in0=hc, scalar=mvs[c][:, 3:4], in1=b, op0=ALU.mult, op1=ALU.add), reads=[Bsc[c], Bp, B_h[c]], writes=[B_h[c]])

    def make_T_from_h(cvr, NT, T, dstT, B_dst):
        hb = [cvr.take([128, D], BF16) for _ in range(2)]
        Bhb = [Buf("hb0"), Buf("hb1")]
        for c in range(NT):
            k = c % 2
            act.op(mk("copy", out=hb[k], in_=h[:, c, :]), reads=[B_h[c]], writes=[Bhb[k]])
            transpose_cols(hb[k], Bhb[k], 16, lambda j0, n, c=c: dstT[:, j0:j0 + n, c * 128:(c + 1) * 128], B_dst, [dve, act])

    def dbg_dump(name, src_ap, rows, reads):
        if name in dbg:
            sp.op(mk("dma_start", out=dbg[name][rows], in_=src_ap), reads=reads)

    def main_loop():
      c0 = 0
      checkpoint(0)
      for ti, NT in enumerate(tiles):
          T = NT * 128
          nkc_tile = c0 + NT
          Lk = nkc_tile * 128
          AT = bufA[:, 0:16 * T].rearrange("p (k t) -> p k t", t=T)
          BT = bufB[:, 0:16 * T].rearrange("p (k t) -> p k t", t=T)
          A_tok = bufA[:, 0:NT * D].rearrange("p (c d) -> p c d", d=D)

          P.barrier()
          cv = Carver()
          for c in range(NT):
              sp.op(mk("dma_start", out=h[:, c, :], in_=xin[(c0 + c) * 128:(c0 + c + 1) * 128, :]), writes=[B_h[c]])
          sp.op(mk("dma_start", out=rope_t[:, 0:NT, :], in_=rope_d[c0 * 128:(c0 + NT) * 128, :].rearrange("(c p) f -> p c f", p=128)), writes=[B_rope])
          g_, b_, Bp = load_ln_params(cv, 0, 1)
          stats = cv.take([128, 4, 6], F32)
          mv = cv.take([128, 8], F32)
          Bs = Buf("lnstat")
          layer_norm_tile(NT, g_, b_, Bp, cv)
          make_T_from_h(cv, NT, T, AT, B_A)
          if ti == 0:
              for c in range(NT):
                  dbg_dump("h0", h[:, c, :], (slice(c * 128, (c + 1) * 128), slice(None)), [B_h[c]])

          for l in range(depth):
              P.barrier()
              checkpoint(1)
              cv = Carver()
              stage = [cv.take([128, NT, 512], BF16) for _ in range(2)]
              B_stage = [Buf("stage0"), Buf("stage1")]
              stageT = [cv.take([128, 4, T], BF16) for _ in range(2)]
              B_stageT = [Buf("stageT0"), Buf("stageT1")]
              stageV = [cv.take([128, 4, NT, 129], BF16) for _ in range(2)]
              B_stageV = [Buf("stageV0"), Buf("stageV1")]
              rA = [cv.take([128, 512], F32)] * 2
              rB = [cv.take([128, 512], F32)] * 2
              B_r = [Buf("r0")] * 2
              rk_cnt = [0]

              def rope_epi(fam, ub, c, ps, bps, k):
                  if fam in ("aq", "ak"):
                      half, nb, co, so = 32, 8, 0, 32
                  else:
                      half, nb, co, so = 64, 4, 64, 128
                  j = rk_cnt[0] % 2
                  rk_cnt[0] += 1
                  psv = ps[:].rearrange("p (b two x) -> p b two x", two=2, x=half)
                  Av = rA[j].rearrange("p (b two x) -> p b two x", two=2, x=half)
                  Bv = rB[j].rearrange("p (b two x) -> p b two x", two=2, x=half)
                  cosb = rope_t[:, c, co:co + half].unsqueeze(1).unsqueeze(1).broadcast_to([128, nb, 2, half])
                  sinb = rope_t[:, c, so:so + half].unsqueeze(1).broadcast_to([128, nb, half])
                  dve.op(mk("tensor_tensor", out=Av, in0=psv, in1=cosb, op=ALU.mult), reads=[bps, B_rope], writes=[B_r[j]])
                  dve.op(mk("tensor_tensor", out=Bv[:, :, 0, :], in0=psv[:, :, 1, :], in1=sinb, op=ALU.mult), reads=[bps, B_rope], writes=[B_r[j]])
                  dve.op(mk("tensor_tensor", out=Bv[:, :, 1, :], in0=psv[:, :, 0, :], in1=sinb, op=ALU.mult), reads=[bps, B_rope], writes=[B_r[j]])
                  if fam in ("aq", "ak"):
                      sv = stage[k][:, c, :].rearrange("p (b two x) -> p b two x", two=2, x=half)
                      dve.op(mk("tensor_tensor", out=sv[:, :, 0, :], in0=Av[:, :, 0, :], in1=Bv[:, :, 0, :], op=ALU.subtract), reads=[B_r[j]], writes=[B_stage[k]])
                      dve.op(mk("tensor_tensor", out=sv[:, :, 1, :], in0=Av[:, :, 1, :], in1=Bv[:, :, 1, :], op=ALU.add), reads=[B_r[j]], writes=[B_stage[k]])
                  else:
                      dve.op(mk("tensor_tensor", out=Av[:, :, 0, :], in0=Av[:, :, 0, :], in1=Bv[:, :, 0, :], op=ALU.subtract), reads=[B_r[j]], writes=[B_r[j]])
                      dve.op(mk("tensor_tensor", out=Av[:, :, 1, :], in0=Av[:, :, 1, :], in1=Bv[:, :, 1, :], op=ALU.add), reads=[B_r[j]], writes=[B_r[j]])
                      if fam == "rq":
                          base = CST_QDEC
                      else:
                          base = CST_KINV0 if (c0 + c) == 0 else CST_KINV
                      scl = cst[:, base + ub * 4: base + ub * 4 + 4].unsqueeze(2).broadcast_to([128, 4, 128])
                      a3 = rA[j].rearrange("p (b x) -> p b x", x=128)
                      s3 = stage[k][:, c, :].rearrange("p (b x) -> p b x", x=128)
                      dve.op(mk("tensor_tensor", out=s3, in0=a3, in1=scl, op=ALU.mult), reads=[B_r[j], B_const], writes=[B_stage[k]])

              ucount = [0]

              def run_unit(fam, ub):
                  k = ucount[0] % 2
                  ucount[0] += 1
                  uinfo, W, B_W = take_unit()

                  if fam in ("aq", "ak", "rq", "rk"):
                      def epi(c, ps, bps):
                          rope_epi(fam, ub, c, ps, bps, k)
                          return lambda c=c: transpose_cols(stage[k][:, c, :], B_stage[k], 4, lambda j0, n, c=c: stageT[k][:, j0:j0 + n, c * 128:(c + 1) * 128], B_stageT[k], [act])
                  elif fam == "av":
                      def epi(c, ps, bps):
                          act.op(mk("copy", out=stageV[k][:, :, c, 0:128], in_=ps[:].rearrange("p (a x) -> p a x", x=128)), reads=[bps], writes=[B_stageV[k]])
                          if c == 0:
                              pool.op(mk("memset", stageV[k][:, :, :, 128:129], 1.0), writes=[B_stageV[k]])
                          if (c0 + c) == 0:
                              pool.op(mk("tensor_scalar", out=stageV[k][:, :, 0, :], in0=stageV[k][:, :, 0, :], scalar1=cst[:, CST_VALID:CST_VALID + 1], scalar2=None, op0=ALU.mult),
                                      reads=[B_const, B_stageV[k]], writes=[B_stageV[k]])
                  elif fam == "rv":
                      def epi(c, ps, bps):
                          act.op(mk("copy", out=stage[k][:, c, :], in_=ps[:]), reads=[bps], writes=[B_stage[k]])
                  elif fam == "rg":
                      def epi(c, ps, bps):
                          act.op(mk("activation", out=stage[k][:, c, :], in_=ps[:], func=AF.Silu), reads=[bps], writes=[B_stage[k]])
                  else:
                      def epi(c, ps, bps):
                          act.op(mk("activation", out=stage[k][:, c, :], in_=ps[:], func=AF.Sigmoid), reads=[bps], writes=[B_stage[k]])
                  gemm_tok(AT, B_A, T, NT, W, B_W, 512, epi)
                  if fam == "aq":
                      sp.op(mk("dma_start", out=qT_s[ub * 4:ub * 4 + 4, :, 0:T].rearrange("a p t -> p a t"), in_=stageT[k]), reads=[B_stageT[k]], writes=[B_qT])
                  elif fam == "ak":
                      sp.op(mk("dma_start", out=kT_c[l, ub * 4:ub * 4 + 4, :, c0 * 128:c0 * 128 + T].rearrange("a p t -> p a t"), in_=stageT[k]), reads=[B_stageT[k]], writes=[B_kT])
                  elif fam == "rq":
                      sp.op(mk("dma_start", out=qdT_s[ub * 4:ub * 4 + 4, :, 0:T].rearrange("a p t -> p a t"), in_=stageT[k]), reads=[B_stageT[k]], writes=[B_qdT])
                  elif fam == "rk":
                      sp.op(mk("dma_start", out=kiT_s[ub * 4:ub * 4 + 4, :, 0:T].rearrange("a p t -> p a t"), in_=stageT[k]), reads=[B_stageT[k]], writes=[B_kiT])
                      sp.op(mk("dma_start", out=ki_s[0:T, ub * 512:(ub + 1) * 512].rearrange("(c p) f -> p c f", p=128), in_=stage[k]), reads=[B_stage[k]], writes=[B_ki])
                  elif fam == "av":
                      sp.op(mk("dma_start", out=v_c[l, ub * 4:ub * 4 + 4, :, c0 * 129:(c0 + NT) * 129].rearrange("a p x -> p a x"),
                                                  in_=stageV[k].rearrange("p a c x -> p a (c x)")), reads=[B_stageV[k]], writes=[B_v])
                  else:
                      dst = {"rv": (rv_s, B_rv), "rg": (sg_s, B_sg), "ga": (sga_s, B_sga), "gb": (sgb_s, B_sgb)}[fam]
                      sp.op(mk("dma_start", out=dst[0][0:T, ub * 512:(ub + 1) * 512].rearrange("(c p) f -> p c f", p=128), in_=stage[k]), reads=[B_stage[k]], writes=[dst[1]])

              fams = [("aq", 4), ("ak", 4), ("av", 4), ("rq", 2), ("rk", 2), ("rv", 4), ("rg", 4), ("ga", 4), ("gb", 4)]
              for fam, n in fams:
                  for ub in range(n):
                      run_unit(fam, ub)

              P.barrier()
              checkpoint(2)
              cv = Carver()
              qTh0 = [cv.take([128, T], BF16) for _ in range(2)]
              qTh1 = [cv.take([128, T], BF16) for _ in range(2)]
              kTh = [cv.take([128, Lk], BF16) for _ in range(2)]
              Vh = [cv.take([128, nkc_tile, 129], BF16) for _ in range(2)]
              B_ld = [Buf("attld0"), Buf("attld1")]
              for k_ in range(2):
                  pool.op(mk("memset", qTh0[k_][64:128, :], 0.0), writes=[B_ld[k_]])
                  pool.op(mk("memset", qTh1[k_][0:64, :], 0.0), writes=[B_ld[k_]])
              NE = 3
              Et = [cv.take([128, 512], BF16) for _ in range(NE)]
              B_E = [Buf(f"E{i}") for i in range(NE)]
              junk = cv.take([128, 128], F32)
              ecnt = [0]
              ocnt = [0]
              ps_groups["acc"] = [0, 1]
              ps_groups["mm"] = [2, 3, 4, 5]
              ps_rr["acc"] = 0
              ps_rr["mm"] = 0
              items = []
              for a in range(16):
                  for i in range(NT):
                      gi = c0 + i
                      for m in range(2):
                          g0s = list(range(0, gi + 1, 4))
                          for g0 in g0s:
                              items.append((a, i, m, g0, g0 == g0s[-1]))
              loaded = set()
              Ocur = {}

              def emit_S(it):
                  a, i, m, g0, last = it
                  k = a % 2
                  gi = c0 + i
                  if a not in loaded:
                      loaded.add(a)
                      sp.op(mk("dma_start", out=qTh0[k][0:64, :], in_=qT_s[a, 0:64, 0:T]), reads=[B_qT], writes=[B_ld[k]])
                      sp.op(mk("dma_start", out=qTh1[k][64:128, :], in_=qT_s[a, 64:128, 0:T]), reads=[B_qT], writes=[B_ld[k]])
                      sp.op(mk("dma_start", out=kTh[k], in_=kT_c[l, a, :, 0:Lk]), reads=[B_kT], writes=[B_ld[k]])
                      sp.op(mk("dma_start", out=Vh[k].rearrange("p c x -> p (c x)"), in_=v_c[l, a, :, 0:nkc_tile * 129]), reads=[B_v], writes=[B_ld[k]])
                  pr = slice(m * 64, (m + 1) * 64)
                  js = list(range(g0, min(g0 + 4, gi + 1)))
                  S, bS = next_ps("mm")
                  for jj, j in enumerate(js):
                      pe.op(mk("matmul", S[:, jj * 128:(jj + 1) * 128], lhsT=kTh[k][:, j * 128:(j + 1) * 128],
                               rhs=(qTh0 if m == 0 else qTh1)[k][:, i * 128:(i + 1) * 128], start=True, stop=True),
                            reads=[B_ld[k]], writes=[bS], signal=(jj == len(js) - 1))
                  return (S, bS, js)

              def emit_rest(it, st):
                  a, i, m, g0, last = it
                  S, bS, js = st
                  k = a % 2
                  gi = c0 + i
                  if g0 == 0:
                      Ocur[m] = next_ps("acc")
                  O, bO = Ocur[m]
                  ei = ecnt[0] % NE
                  ecnt[0] += 1
                  E, bE = Et[ei], B_E[ei]
                  n = len(js)
                  act.op(mk("activation", out=E[:, 0:n * 128], in_=S[:, 0:n * 128], func=AF.Exp, scale=0.125), reads=[bS], writes=[bE])
                  if gi in js:
                      jj = js.index(gi)
                      dve.op(mk("tensor_tensor", out=E[:, jj * 128:(jj + 1) * 128], in0=E[:, jj * 128:(jj + 1) * 128], in1=tri_bf[:], op=ALU.mult),
                             reads=[bE, B_const], writes=[bE])
                  for jj, j in enumerate(js):
                      pe.op(mk("matmul", O[:, 0:129], lhsT=E[:, jj * 128:(jj + 1) * 128], rhs=Vh[k][:, j, :],
                               start=(j == 0), stop=(j == gi)),
                            reads=[bE, B_ld[k]], writes=[bO], signal=(j == gi))
                  if not (last and m == 1):
                      return
                  oi = ocnt[0] % NOR
                  ocnt[0] += 1
                  (O0, bO0), (O1, bO1) = Ocur[0], Ocur[1]
                  s_ = sc[oi]
                  o_ = oA[oi]
                  Bo = B_o[oi]
                  dve.op(mk("tensor_scalar", out=s_[:, 0:1], in0=O0[:, 128:129], scalar1=cst[:, CST_TINY:CST_TINY + 1], scalar2=None, op0=ALU.max), reads=[bO0, B_const], writes=[Bo])
                  dve.op(mk("tensor_scalar", out=s_[:, 1:2], in0=O1[:, 128:129], scalar1=cst[:, CST_TINY:CST_TINY + 1], scalar2=None, op0=ALU.max), reads=[bO1, B_const], writes=[Bo])
                  dve.op(mk("reciprocal", out=s_[:, 2:4], in_=s_[:, 0:2]), reads=[Bo], writes=[Bo])
                  dve.op(mk("tensor_tensor", out=s_[:, 4:5], in0=s_[:, 3:4], in1=lam_t[:, l, 0:1], op=ALU.mult), reads=[Bo, B_lay], writes=[Bo])
                  dve.op(mk("tensor_scalar", out=o_, in0=O0[:, 0:128], scalar1=s_[:, 2:3], scalar2=None, op0=ALU.mult), reads=[bO0, Bo], writes=[Bo])
                  dve.op(mk("scalar_tensor_tensor", out=o_, in0=O1[:, 0:128], scalar=s_[:, 4:5], in1=o_, op0=ALU.mult, op1=ALU.add), reads=[bO1, Bo], writes=[Bo])
                  dve.op(mk("scalar_tensor_tensor", out=junk, in0=o_, scalar=1.0, in1=o_, op0=ALU.mult, op1=ALU.mult, accum_out=s_[:, 5:6]), reads=[Bo], writes=[Bo, B_junk])
                  y_ = Yb[oi]

                  def part2(s_=s_, o_=o_, y_=y_, Bo=Bo):
                      pool.op(mk("tensor_scalar", out=s_[:, 6:7], in0=s_[:, 5:6], scalar1=1.0 / 128.0, scalar2=EPS, op0=ALU.mult, op1=ALU.add), reads=[Bo], writes=[Bo])
                      pool.op(mk("tensor_tensor", out=s_[:, 7:8], in0=s_[:, 6:7], in1=cst[:, CST_NHALF:CST_NHALF + 1], op=ALU.pow), reads=[Bo, B_const], writes=[Bo])
                      dve.op(mk("scalar_tensor_tensor", out=y_, in0=o_, scalar=s_[:, 7:8], in1=subg_bc[:, l, :], op0=ALU.mult, op1=ALU.mult), reads=[Bo, B_lay], writes=[Bo])

                  def part3(y_=y_, Bo=Bo, a=a, i=i):
                      transpose_cols(y_, Bo, 1, lambda j0, n, a=a, i=i: BT[:, a:a + 1, i * 128:(i + 1) * 128], B_B, [dve])
                  etails.append([2, part2])
                  etails.append([4, part3])

              def run_tails(force=False):
                  keep = []
                  for t in etails:
                      t[0] -= 1
                  for t in list(etails):
                      if force or t[0] <= 0:
                          t[1]()
                          etails.remove(t)

              etails = []
              NOR = 6
              oA = [cv.take([128, 128], F32) for _ in range(NOR)]
              Yb = [cv.take([128, 128], BF16) for _ in range(NOR)]
              sc = [cv.take([128, 8], F32) for _ in range(NOR)]
              B_o = [Buf(f"o{x}") for x in range(NOR)]
              B_junk = Buf("junk")
              DEPTH_PIPE = 3
              pipe = []
              for it in items:
                  pipe.append((it, emit_S(it)))
                  if len(pipe) > DEPTH_PIPE:
                      emit_rest(*pipe.pop(0))
                      run_tails()
              while pipe:
                  emit_rest(*pipe.pop(0))
                  run_tails()
              while etails:
                  run_tails(force=True)
              ps_groups["acc"] = [0, 1, 2, 3]
              ps_groups["mm"] = [4, 5]
              if "oaT" in dbg and ti == 0 and l == 0:
                  P.barrier()
                  tmpf = cv.take([128, T], F32)
                  for kc in range(16):
                      Bt = Buf("t")
                      dve.op(mk("tensor_copy", out=tmpf, in_=BT[:, kc, :]), reads=[B_B], writes=[Bt])
                      sp.op(mk("dma_start", out=dbg["oaT"][kc * 128:(kc + 1) * 128, 0:T], in_=tmpf), reads=[Bt])
                      P.barrier()

              P.barrier()
              checkpoint(3)
              cv = Carver()
              gst = [cv.take([128, NT, 512], BF16) for _ in range(2)]
              B_gst = [Buf("gst0"), Buf("gst1")]
              tmpf = [cv.take([128, 512], F32) for _ in range(2)]
              B_tmpf = [Buf("tmpf0"), Buf("tmpf1")]
              for u in range(4):
                  k = u % 2
                  uinfo, W, B_W = take_unit()
                  sp.op(mk("dma_start", out=gst[k], in_=sga_s[0:T, u * 512:(u + 1) * 512].rearrange("(c p) f -> p c f", p=128)), reads=[B_sga], writes=[B_gst[k]])

                  def epi(c, ps, bps, u=u, k=k):
                      dve.op(mk("tensor_tensor", out=A_tok[:, c, u * 512:(u + 1) * 512], in0=ps[:], in1=gst[k][:, c, :], op=ALU.mult), reads=[bps, B_gst[k]], writes=[B_A])
                  gemm_tok(BT, B_B, T, NT, W, B_W, 512, epi)

              P.barrier()
              checkpoint(4)
              cv = Carver()
              retg_bc = cv.take([128, D], F32)
              sp.op(mk("dma_start", out=retg_bc, in_=retg[l:l + 1, :].partition_broadcast(128)[:, 0, :]), writes=[B_lay])
              qdh = [cv.take([128, T], BF16) for _ in range(2)]
              kih = [cv.take([128, T], BF16) for _ in range(2)]
              kth = [cv.take([128, NT, 128], BF16) for _ in range(2)]
              rvh = [cv.take([128, NT, 256], BF16) for _ in range(2)]
              sgh = [cv.take([128, NT, 256], BF16) for _ in range(2)]
              B_rl = [Buf("rl0"), Buf("rl1")]
              Sbf = cv.take([128, 256], BF16)
              B_Sbf = Buf("Sbf")
              NRR = 4
              Pm = [cv.take([128, 128], BF16) for _ in range(NRR)]
              B_Pm = [Buf(f"Pm{x}") for x in range(NRR)]
              gs = [cv.take([128, 16], F32) for _ in range(NRR)]
              yf = [cv.take([128, 256], F32) for _ in range(NRR)]
              yb = [cv.take([128, 256], BF16) for _ in range(NRR)]
              B_y = [Buf(f"y{x}") for x in range(NRR)]
              rc = [0]
              rtails = []
              for r in range(8):
                  k = r % 2
                  sp.op(mk("dma_start", out=qdh[k], in_=qdT_s[r, :, 0:T]), reads=[B_qdT], writes=[B_rl[k]])
                  sp.op(mk("dma_start", out=kih[k], in_=kiT_s[r, :, 0:T]), reads=[B_kiT], writes=[B_rl[k]])
                  sp.op(mk("dma_start", out=kth[k], in_=ki_s[0:T, r * 128:(r + 1) * 128].rearrange("(c p) f -> p c f", p=128)), reads=[B_ki], writes=[B_rl[k]])
                  sp.op(mk("dma_start", out=rvh[k], in_=rv_s[0:T, r * 256:(r + 1) * 256].rearrange("(c p) f -> p c f", p=128)), reads=[B_rv], writes=[B_rl[k]])
                  sp.op(mk("dma_start", out=sgh[k], in_=sg_s[0:T, r * 256:(r + 1) * 256].rearrange("(c p) f -> p c f", p=128)), reads=[B_sg], writes=[B_rl[k]])
                  St = state[:, l, r, :]
                  dve.op(mk("tensor_copy", out=Sbf, in_=St), reads=[B_state], writes=[B_Sbf])
                  for i in range(NT):
                      j = rc[0] % NRR
                      rc[0] += 1
                      cs = slice(i * 128, (i + 1) * 128)
                      S, bS = next_ps("mm")
                      pe.op(mk("matmul", S[:, 0:128], lhsT=kih[k][:, cs], rhs=qdh[k][:, cs], start=True, stop=True), reads=[B_rl[k]], writes=[bS])
                      dve.op(mk("tensor_tensor", out=Pm[j], in0=S[:, 0:128], in1=tri_bf[:], op=ALU.mult), reads=[bS, B_const], writes=[B_Pm[j]])
                      O, bO = next_ps()
                      pe.op(mk("matmul", O[:, 0:256], lhsT=Pm[j], rhs=rvh[k][:, i, :], start=True, stop=False), reads=[B_Pm[j], B_rl[k]], writes=[bO], signal=False)
                      pe.op(mk("matmul", O[:, 0:256], lhsT=qdh[k][:, cs], rhs=Sbf, start=False, stop=True), reads=[B_Sbf, B_rl[k]], writes=[bO])
                      U, bU = next_ps()
                      pe.op(mk("matmul", U[:, 0:256], lhsT=kth[k][:, i, :], rhs=rvh[k][:, i, :], start=True, stop=True), reads=[B_rl[k]], writes=[bU])
                      gcol = cst[:, CST_G128 + r:CST_G128 + r + 1]
                      dve.op(mk("tensor_scalar", out=St, in0=St, scalar1=gcol, scalar2=None, op0=ALU.mult), reads=[B_state, B_const], writes=[B_state])
                      dve.op(mk("scalar_tensor_tensor", out=St, in0=U[:, 0:256], scalar=gcol, in1=St, op0=ALU.mult, op1=ALU.add), reads=[bU, B_state, B_const], writes=[B_state])
                      dve.op(mk("tensor_copy", out=Sbf, in_=St), reads=[B_state], writes=[B_Sbf])
                      g_s = gs[j]
                      dve.op(mk("bn_stats", out=g_s[:, 0:6], in_=O[:, 0:256]), reads=[bO], writes=[B_y[j]])
                      dve.op(mk("bn_aggr", out=g_s[:, 6:8], in_=g_s[:, 0:6]), reads=[B_y[j]], writes=[B_y[j]])
                      act.op(mk("activation", out=g_s[:, 8:9], in_=g_s[:, 7:8], func=AF.Sqrt, bias=cst[:, CST_EPS:CST_EPS + 1], scale=1.0), reads=[B_y[j], B_const], writes=[B_y[j]])
                      dve.op(mk("reciprocal", out=g_s[:, 9:10], in_=g_s[:, 8:9]), reads=[B_y[j]], writes=[B_y[j]])
                      dve.op(mk("scalar_tensor_tensor", out=yf[j], in0=O[:, 0:256], scalar=g_s[:, 6:7], in1=retg_bc[:, r * 256:(r + 1) * 256], op0=ALU.subtract, op1=ALU.mult), reads=[bO, B_y[j], B_lay], writes=[B_y[j]])
                      dve.op(mk("scalar_tensor_tensor", out=yb[j], in0=yf[j], scalar=g_s[:, 9:10], in1=sgh[k][:, i, :], op0=ALU.mult, op1=ALU.mult), reads=[B_y[j], B_rl[k]], writes=[B_y[j]])
                      rtails.append(lambda j=j, r=r, i=i: transpose_cols(yb[j], B_y[j], 2, lambda j0, n, r=r, i=i: BT[:, 2 * r + j0:2 * r + j0 + n, i * 128:(i + 1) * 128], B_B, [act]))
                      if len(rtails) > 2:
                          rtails.pop(0)()
              while rtails:
                  rtails.pop(0)()

              P.barrier()
              checkpoint(5)
              cv = Carver()
              gst = [cv.take([128, NT, 512], BF16) for _ in range(2)]
              B_gst = [Buf("gst0"), Buf("gst1")]
              tmpf = [cv.take([128, 512], F32) for _ in range(2)]
              B_tmpf = [Buf("tmpf0"), Buf("tmpf1")]
              tc_ = [0]
              for u in range(4):
                  k = u % 2
                  uinfo, W, B_W = take_unit()
                  sp.op(mk("dma_start", out=gst[k], in_=sgb_s[0:T, u * 512:(u + 1) * 512].rearrange("(c p) f -> p c f", p=128)), reads=[B_sgb], writes=[B_gst[k]])

                  def epi(c, ps, bps, u=u, k=k):
                      j = tc_[0] % 2
                      tc_[0] += 1
                      dve.op(mk("tensor_tensor", out=tmpf[j], in0=ps[:], in1=gst[k][:, c, :], op=ALU.mult), reads=[bps, B_gst[k]], writes=[B_tmpf[j]])
                      pool.op(mk("tensor_tensor", out=A_tok[:, c, u * 512:(u + 1) * 512], in0=tmpf[j], in1=A_tok[:, c, u * 512:(u + 1) * 512], op=ALU.add), reads=[B_tmpf[j], B_A], writes=[B_A])
                  gemm_tok(BT, B_B, T, NT, W, B_W, 512, epi)
              P.barrier()
              checkpoint(6)
              for c in range(NT):
                  transpose_cols(A_tok[:, c, :], B_A, 16, lambda j0, n, c=c: BT[:, j0:j0 + n, c * 128:(c + 1) * 128], B_B, [dve, act])

              P.barrier()
              checkpoint(7)
              cv = Carver()
              g_, b_, Bp = load_ln_params(cv, 2 + 4 * l, 3 + 4 * l)
              stats = cv.take([128, 4, 6], F32)
              mv = cv.take([128, 8], F32)
              Bs = Buf("lnstat")
              for u in range(4):
                  uinfo, W, B_W = take_unit()

                  def epi(c, ps, bps, u=u):
                      hs = h[:, c, u * 512:(u + 1) * 512]
                      dve.op(mk("scalar_tensor_tensor", out=hs, in0=hs, scalar=alpha, in1=ps[:], op0=ALU.mult, op1=ALU.add), reads=[bps, B_h[c]], writes=[B_h[c]])
                  gemm_tok(BT, B_B, T, NT, W, B_W, 512, epi)
              layer_norm_tile(NT, g_, b_, Bp, cv)
              make_T_from_h(cv, NT, T, AT, B_A)
              if ti == 0 and l == 0:
                  for c in range(NT):
                      dbg_dump("h1", h[:, c, :], (slice(c * 128, (c + 1) * 128), slice(None)), [B_h[c]])
              for c in range(NT):
                  act.op(mk("activation", out=h[:, c, :], in_=h[:, c, :], func=AF.Copy, scale=alpha), reads=[B_h[c]], writes=[B_h[c]])

              P.barrier()
              checkpoint(8)
              cv = Carver()
              nblk = (T + 511) // 512
              bs = T // nblk
              assert bs * nblk == T
              actT = [cv.take([128, 4, T], BF16) for _ in range(2)]
              B_actT = [Buf("actT0"), Buf("actT1")]
              gl = cv.take([128, 4, T], F32)
              B_gl = Buf("gl")
              ug = [cv.take([128, T + 2], F32) for _ in range(2)]
              B_u = [Buf("u0"), Buf("u1")]
              t1 = [cv.take([128, T], F32) for _ in range(2)]
              B_t1 = [Buf("t10"), Buf("t11")]
              uc = [0]

              def emit_wdown(g):
                  nf = min(4, NFC - 4 * g)
                  ka = g % 2
                  _, Wd, B_Wd = take_unit()
                  Wdv = Wd[:, 0:nf * 2048].rearrange("p (f c) -> p f c", c=2048)
                  for c in range(NT):
                      pump(1)
                      for n in range(4):
                          ps, bps = next_ps("mm")
                          for f in range(nf):
                              pe.op(mk("matmul", ps[:], lhsT=actT[ka][:, f, c * 128:(c + 1) * 128], rhs=Wdv[:, f, n * 512:(n + 1) * 512],
                                       start=(f == 0), stop=(f == nf - 1)),
                                    reads=[B_actT[ka], B_Wd[f]], writes=[bps], signal=(f == nf - 1))
                          hs = h[:, c, n * 512:(n + 1) * 512]
                          dve.op(mk("tensor_tensor", out=hs, in0=hs, in1=ps[:], op=ALU.add), reads=[bps, B_h[c]], writes=[B_h[c]])

              for g in range(ngrp):
                  nf = min(4, NFC - 4 * g)
                  ka = g % 2
                  ncol = nf * 128
                  for w in range(2):
                      _, Wt, B_Wt = take_unit()
                      Wv_ = Wt[:, 0:16 * ncol].rearrange("p (k c) -> p k c", c=ncol)
                      for f in range(nf):
                          fc = g * 4 + f
                          ci = fc + w * NFC
                          j = uc[0] % 2
                          uc[0] += 1
                          for tb in range(nblk):
                              ps, bps = next_ps()
                              for kc in range(16):
                                  pe.op(mk("matmul", ps[:, 0:bs], lhsT=Wv_[:, kc, f * 128:(f + 1) * 128], rhs=AT[:, kc, tb * bs:(tb + 1) * bs],
                                           start=(kc == 0), stop=(kc == 15)),
                                        reads=[B_A, B_Wt[kc // 4]], writes=[bps], signal=(kc == 15))
                              act.op(mk("copy", out=ug[j][:, 2 + tb * bs:2 + (tb + 1) * bs], in_=ps[:, 0:bs]), reads=[bps], writes=[B_u[j]])
                          pump(1)
                          pool.op(mk("tensor_copy", out=ug[j][:, 0:2], in_=halo[:, l, ci, :]), reads=[B_halo], writes=[B_u[j]])
                          if ti == 0:
                              pool.op(mk("memset", ug[j][:, 2:2 + PAD], 0.0), writes=[B_u[j]])
                          pool.op(mk("tensor_copy", out=halo[:, l, ci, :], in_=ug[j][:, T:T + 2]), reads=[B_u[j]], writes=[B_halo])
                          dve.op(mk("tensor_scalar", out=t1[j], in0=ug[j][:, 0:T], scalar1=cw[:, l, 0, ci:ci + 1], scalar2=cw[:, l, 3, ci:ci + 1], op0=ALU.mult, op1=ALU.add),
                                 reads=[B_u[j], B_lay], writes=[B_t1[j]])
                          dve.op(mk("scalar_tensor_tensor", out=t1[j], in0=ug[j][:, 1:T + 1], scalar=cw[:, l, 1, ci:ci + 1], in1=t1[j], op0=ALU.mult, op1=ALU.add),
                                 reads=[B_u[j], B_lay, B_t1[j]], writes=[B_t1[j]])
                          dve.op(mk("scalar_tensor_tensor", out=t1[j], in0=ug[j][:, 2:T + 2], scalar=cw[:, l, 2, ci:ci + 1], in1=t1[j], op0=ALU.mult, op1=ALU.add),
                                 reads=[B_u[j], B_lay, B_t1[j]], writes=[B_t1[j]])
                          if w == 0:
                              act.op(mk("activation", out=gl[:, f, :], in_=t1[j], func=AF.Gelu), reads=[B_t1[j]], writes=[B_gl])
                          else:
                              pool.op(mk("tensor_tensor", out=actT[ka][:, f, :], in0=gl[:, f, :], in1=t1[j], op=ALU.mult), reads=[B_gl, B_t1[j]], writes=[B_actT[ka]])
                  if g > 0:
                      emit_wdown(g - 1)
              emit_wdown(ngrp - 1)

              P.barrier()
              checkpoint(9)
              if ti == 0 and l == 0:
                  for c in range(NT):
                      dbg_dump("h2p", h[:, c, :], (slice(c * 128, (c + 1) * 128), slice(None)), [B_h[c]])
              cv = Carver()
              g_, b_, Bp = load_ln_params(cv, 4 + 4 * l, 5 + 4 * l)
              stats = cv.take([128, 4, 6], F32)
              mv = cv.take([128, 8], F32)
              Bs = Buf("lnstat")
              layer_norm_tile(NT, g_, b_, Bp, cv)
              if l + 1 < depth:
                  make_T_from_h(cv, NT, T, AT, B_A)
              if ti == 0 and l == 0:
                  for c in range(NT):
                      dbg_dump("h2", h[:, c, :], (slice(c * 128, (c + 1) * 128), slice(None)), [B_h[c]])

          for c in range(NT):
              sp.op(mk("dma_start", out=out_d[(c0 + c) * 128:(c0 + c + 1) * 128, :], in_=h[:, c, :]), reads=[B_h[c]])
          c0 += NT


    try:
        main_loop()
        assert ucur[0] == len(units), (ucur[0], len(units))
    except StopBuild:
        pass
    P.finish()
    return nc


def prep_shared(inputs, depth):
    f = lambda a: np.ascontiguousarray(np.asarray(a, dtype=np.float32))
    lnp = np.concatenate([f(inputs["ln_emb_g"])[None], f(inputs["ln_emb_b"])[None]] +
                         [np.stack([f(inputs["ln1_g"])[l], f(inputs["ln1_b"])[l], f(inputs["ln2_g"])[l], f(inputs["ln2_b"])[l]]) for l in range(depth)], axis=0)
    lamv = np.stack([f(inputs["lam_q1"]), f(inputs["lam_k1"]), f(inputs["lam_q2"]), f(inputs["lam_k2"])], axis=1)
    convw = np.concatenate([f(inputs["conv_w"]), f(inputs["conv_b"])[:, None, :]], axis=1).reshape(depth, 4, 2 * NFC, 128)
    return {
        "w_in": f(inputs["w_in"]), "w_a": f(inputs["w_branch_a"]), "w_b": f(inputs["w_branch_b"]), "w_o": f(inputs["w_out"]),
        "w_up": f(inputs["w_up"]), "w_dn": f(inputs["w_down"]), "lnp": np.ascontiguousarray(lnp),
        "retg": f(inputs["ret_gn_g"]), "subg": f(inputs["subln_g"]), "lamv": np.ascontiguousarray(lamv), "convw": np.ascontiguousarray(convw),
    }


def run(inputs, tiles, depth, core_ids, debug=None, trace=False):
    x = np.asarray(inputs["x"], dtype=np.float32)
    B, S, _ = x.shape
    nch = (S + 128) // 128
    L = nch * 128
    nc = build_program(nch, tiles, depth, debug=debug)
    shared = prep_shared(inputs, depth)
    shared.update(make_consts(nch))
    meta = np.asarray(inputs["meta_tokens"], dtype=np.float32)
    in_maps = []
    for ci in core_ids:
        b = ci % B
        xin = np.zeros((L, D), np.float32)
        xin[PAD:PAD + N_META] = meta
        xin[128:] = x[b]
        m = dict(shared)
        m["xin"] = xin
        in_maps.append(m)
    res = run_bass_kernel_spmd(nc, in_maps, core_ids=list(core_ids), trace=trace)
    return res


def kernel(**inputs):
    depth = int(np.asarray(inputs["w_in"]).shape[0])
    x = np.asarray(inputs["x"])
    B, S, _ = x.shape
    nch = (S + 128) // 128
    tiles = []
    rem = nch
    while rem > 0:
        t = min(5 if not tiles else 4, rem)
        tiles.append(t)
        rem -= t
    res = run(inputs, tiles, depth, list(range(B)))
    out = np.stack([np.asarray(r["out"], dtype=np.float32)[128:] for r in res.results[:B]], axis=0)
    return out
```

```python
import math
import numpy as np
import ml_dtypes
import concourse.bass as bass
import concourse.mybir as mybir
from concourse.bass_utils import run_bass_kernel_spmd

F32 = mybir.dt.float32
BF16 = mybir.dt.bfloat16
AF = mybir.ActivationFunctionType
ALU = mybir.AluOpType

D = 2048
DFF = 5504
NFC = 43
EPS = 1e-5
N_META = 16
PAD = 112
ROPE_THETA = 10000.0


def mk(name, *args, **kw):
    return lambda e: getattr(e, name)(*args, **kw)


class Buf:
    __slots__ = ("name", "lw", "rd")

    def __init__(self, name):
        self.name = name
        self.lw = None
        self.rd = []


class Eng:
    def __init__(self, prog, name, sems, is_dma=False):
        self.prog = prog
        self.name = name
        self.sems = sems
        self.is_dma = is_dma
        self.ops = []
        self.cnt = 0
        self.waited = {}
        self.pend_r = []
        self.pend_w = []

    def _wait(self, ev):
        if ev is None:
            return
        si, val = ev
        if self.waited.get(si, 0) >= val:
            return
        self.waited[si] = val
        sem = self.prog.semlist[si]
        self.ops.append(mk("wait_ge", sem, val))

    def _deps(self, reads, writes):
        for e in self.prog.engs:
            if e is self:
                continue
            for b in writes:
                assert all(b is not x for x in e.pend_r), ("write while unsignaled read pending", b.name, e.name, self.name)
                assert all(b is not x for x in e.pend_w), ("write while unsignaled write pending", b.name, e.name, self.name)
            for b in reads:
                assert all(b is not x for x in e.pend_w), ("read while unsignaled write pending", b.name, e.name, self.name)
        for b in reads:
            self._wait(b.lw)
        for b in writes:
            self._wait(b.lw)
            for ev in b.rd:
                self._wait(ev)

    def op(self, fn, reads=(), writes=(), signal=True, after=()):
        self._deps(reads, writes)
        for b in after:
            for ev in b.rd:
                self._wait(ev)
        if self.is_dma:
            K = len(self.sems)
            n = self.cnt
            self.cnt += 1
            slot = n % K
            need = 16 * (n // K)
            si = self.sems[slot]
            self._wait((si, need))
            ev = (si, need + 16)
            sem = self.prog.semlist[si]
            self.ops.append(lambda e, fn=fn, sem=sem: fn(e).then_inc(sem, 16))
        elif signal:
            self.cnt += 1
            si = self.sems[0]
            ev = (si, self.cnt)
            sem = self.prog.semlist[si]
            self.ops.append(lambda e, fn=fn, sem=sem: fn(e).then_inc(sem, 1))
        else:
            self.ops.append(lambda e, fn=fn: fn(e))
            self.pend_r.extend(reads)
            self.pend_w.extend(writes)
            return None
        for b in list(reads) + self.pend_r:
            b.rd.append(ev)
        for b in list(writes) + self.pend_w:
            b.lw = ev
            b.rd = []
        self.pend_r = []
        self.pend_w = []
        return ev

    def last_event(self):
        if self.is_dma:
            evs = []
            K = len(self.sems)
            for slot in range(K):
                cnt = (self.cnt - slot + K - 1) // K if self.cnt > slot else 0
                if cnt > 0:
                    evs.append((self.sems[slot], 16 * cnt))
            return evs
        if self.cnt == 0:
            return []
        return [(self.sems[0], self.cnt)]


class Prog:
    def __init__(self, nc):
        self.nc = nc
        self.semlist = []

        def mk(n, name):
            ids = []
            for i in range(n):
                self.semlist.append(nc.alloc_semaphore(f"s_{name}{i}"))
                ids.append(len(self.semlist) - 1)
            return ids

        self.pe = Eng(self, "pe", mk(1, "pe"))
        self.act = Eng(self, "act", mk(1, "act"))
        self.dve = Eng(self, "dve", mk(1, "dve"))
        self.pool = Eng(self, "pool", mk(1, "pool"))
        self.sp = Eng(self, "sp", mk(12, "sp"), is_dma=True)
        self.engs = [self.pe, self.act, self.dve, self.pool, self.sp]

    def barrier(self):
        evs = []
        for e in self.engs:
            assert not e.pend_r and not e.pend_w, e.name
            evs.extend(e.last_event())
        for e in self.engs:
            for ev in evs:
                e._wait(ev)

    def finish(self):
        self.barrier()
        nc = self.nc
        with nc.Block() as block:
            @block.sync
            def _(e):
                for f in self.sp.ops:
                    f(e)

            @block.tensor
            def _(e):
                for f in self.pe.ops:
                    f(e)

            @block.scalar
            def _(e):
                for f in self.act.ops:
                    f(e)

            @block.vector
            def _(e):
                for f in self.dve.ops:
                    f(e)

            @block.gpsimd
            def _(e):
                for f in self.pool.ops:
                    f(e)


def make_consts(nch):
    L = nch * 128
    pos = (np.arange(L) - PAD).astype(np.float32)
    a_freq = (1.0 / (ROPE_THETA ** (np.arange(0, 64, 2, dtype=np.float32) / 64))).astype(np.float32)
    r_freq = (1.0 / (ROPE_THETA ** np.linspace(0.0, 1.0, 64, dtype=np.float32))).astype(np.float32)
    angA = (pos[:, None] * a_freq[None, :]).astype(np.float32)
    angR = (pos[:, None] * r_freq[None, :]).astype(np.float32)
    rope = np.concatenate([np.cos(angA), np.sin(angA), np.cos(angR), np.sin(angR)], axis=1).astype(np.float32)
    p = np.arange(128, dtype=np.float64)
    lg = np.log(1.0 - 2.0 ** (-5.0 - np.arange(8, dtype=np.float64)))
    qdec = np.exp(lg[None, :] * (p[:, None] + 1.0))
    kinv = np.exp(-lg[None, :] * (p[:, None] + 1.0)) * (128.0 ** -0.5)
    valid = (p >= PAD).astype(np.float64)[:, None]
    kinv0 = kinv * valid
    g128 = np.broadcast_to(np.exp(lg * 128.0)[None, :], (128, 8))
    cst = np.concatenate([qdec, kinv, kinv0, g128, valid, np.full((128, 1), EPS), np.full((128, 1), 1e-30), np.full((128, 1), -0.5)], axis=1).astype(np.float32)
    ident = np.eye(128, dtype=np.float32)
    tri = (np.arange(128)[:, None] <= np.arange(128)[None, :]).astype(np.float32)
    return {
        "rope": rope,
        "cst": cst,
        "ident_bf": ident.astype(ml_dtypes.bfloat16),
        "ident_f": ident,
        "tri_bf": tri.astype(ml_dtypes.bfloat16),
    }


CST_QDEC, CST_KINV, CST_KINV0, CST_G128, CST_VALID, CST_EPS, CST_TINY, CST_NHALF = 0, 8, 16, 24, 32, 33, 34, 35
CST_N = 36


def build_program(nch, tiles, depth, debug=None):
    assert sum(tiles) == nch
    L = nch * 128
    NTM = max(tiles)
    TM = NTM * 128
    alpha = float((2 * depth) ** 0.25)

    nc = bass.Bass("TRN2", target_bir_lowering=False)
    P = Prog(nc)
    import os
    STOP = int(os.environ.get("KSTOP", "99"))

    class StopBuild(Exception):
        pass

    def checkpoint(k):
        if STOP <= k:
            raise StopBuild()
    pe, act, dve, pool, sp = P.pe, P.act, P.dve, P.pool, P.sp

    def din(name, shape, dt=F32):
        return nc.dram_tensor(name, list(shape), dt, kind="ExternalInput").ap()

    def dscr(name, shape, dt=BF16):
        return nc.dram_tensor(name, list(shape), dt, kind="Internal").ap()

    xin = din("xin", [L, D])
    w_in = din("w_in", [depth, D, 16384])
    w_a = din("w_a", [depth, D, D])
    w_b = din("w_b", [depth, D, D])
    w_o = din("w_o", [depth, D, D])
    w_up = din("w_up", [depth, D, 2 * DFF])
    w_dn = din("w_dn", [depth, DFF, D])
    lnp = din("lnp", [2 + 4 * depth, D])
    retg = din("retg", [depth, D])
    subg = din("subg", [depth, 128])
    lamv = din("lamv", [depth, 4, 64])
    convw = din("convw", [depth, 4, 2 * NFC, 128])
    rope_d = din("rope", [L, 192])
    cst_d = din("cst", [128, CST_N])
    identbf_d = din("ident_bf", [128, 128], BF16)
    identf_d = din("ident_f", [128, 128])
    tribf_d = din("tri_bf", [128, 128], BF16)
    out_d = nc.dram_tensor("out", [L, D], F32, kind="ExternalOutput").ap()

    qT_s = dscr("qT_s", [16, 128, TM])
    kT_c = dscr("kT_c", [depth, 16, 128, L])
    v_c = dscr("v_c", [depth, 16, 128, nch * 129])
    qdT_s = dscr("qdT_s", [8, 128, TM])
    kiT_s = dscr("kiT_s", [8, 128, TM])
    ki_s = dscr("ki_s", [TM, 1024])
    rv_s = dscr("rv_s", [TM, D])
    sg_s = dscr("sg_s", [TM, D])
    sga_s = dscr("sga_s", [TM, D])
    sgb_s = dscr("sgb_s", [TM, D])
    UPL = 44 + 3 * ((NFC + 3) // 4)
    NUP = depth * UPL
    wcache_l = [dscr(f"wcache{l}", [UPL, 128, 8192]) for l in range(depth)]
    B_wc = [Buf(f"wc{i}") for i in range(NUP)]
    B_qT, B_kT, B_v = Buf("qT_s"), Buf("kT_c"), Buf("v_c")
    B_qdT, B_kiT, B_ki, B_rv, B_sg, B_sga, B_sgb = (Buf(n) for n in ["qdT", "kiT", "ki", "rv", "sg", "sga", "sgb"])

    dbg = {}
    if debug:
        for name, shape in debug.items():
            dbg[name] = nc.dram_tensor("dbg_" + name, list(shape), F32, kind="ExternalOutput").ap()

    def sb(name, shape, dt):
        return nc.alloc_sbuf_tensor("sb_" + name, list(shape), dt)

    h = sb("h", [128, NTM, D], F32)
    B_h = [Buf(f"h{c}") for c in range(NTM)]
    bufA = sb("bufA", [128, 16 * TM], BF16)
    bufB = sb("bufB", [128, 16 * TM], BF16)
    B_A, B_B = Buf("bufA"), Buf("bufB")
    NST = 4
    wst = [sb(f"wst{i}", [128, 2048], F32) for i in range(NST)]
    B_wst = [Buf(f"wst{i}") for i in range(NST)]
    wbf = [sb(f"wbf{i}", [128, 8192], BF16) for i in range(2)]
    B_wbf = [[Buf(f"wbf{i}_{q}") for q in range(4)] for i in range(2)]
    state = sb("state", [128, depth, 8, 256], F32)
    B_state = Buf("state")
    cst = sb("cst", [128, CST_N], F32)
    ident_bf = sb("ident_bf", [128, 128], BF16)
    ident_f = sb("ident_f", [128, 128], F32)
    tri_bf = sb("tri_bf", [128, 128], BF16)
    B_const = Buf("const")
    cw = sb("cw", [128, depth, 4, 2 * NFC], F32)
    B_cw = Buf("cw")
    halo = sb("halo", [128, depth, 2 * NFC, 2], F32)
    B_halo = Buf("halo")
    subg_bc = sb("subg_bc", [128, depth, 128], F32)
    lam_t = sb("lam_t", [128, depth, 8], F32)
    rope_t = sb("rope_t", [128, NTM, 192], F32)
    B_rope = Buf("rope")
    B_lay = Buf("layerparams")
    SCR_BYTES = 35 * 1024
    scr = sb("scr", [128, SCR_BYTES // 2], BF16)

    psb = [nc.alloc_psum_tensor(f"ps{i}", [128, 512], F32) for i in range(8)]
    B_ps = [Buf(f"ps{i}") for i in range(8)]
    ps_groups = {"acc": [0, 1, 2, 3], "mm": [4, 5], "tp": [6, 7]}
    ps_rr = {"acc": 0, "mm": 0, "tp": 0}

    def next_ps(grp="acc"):
        lst = ps_groups[grp]
        i = lst[ps_rr[grp] % len(lst)]
        ps_rr[grp] += 1
        return psb[i], B_ps[i]

    class Carver:
        def __init__(self):
            self.off = 0

        def take(self, shape, dt):
            n = int(np.prod(shape[1:]))
            nb = n * (4 if dt == F32 else 2)
            nb = (nb + 31) // 32 * 32
            assert self.off + nb <= SCR_BYTES, (self.off, nb, SCR_BYTES)
            ap = scr[:, self.off // 2:(self.off + nb) // 2]
            self.off += nb
            if dt == F32:
                ap = ap.bitcast(F32)
            ap = ap[:, 0:n]
            if len(shape) == 3:
                ap = ap.rearrange("p (a b) -> p a b", b=shape[2])
            elif len(shape) == 4:
                ap = ap.rearrange("p (a b c) -> p a b c", b=shape[2], c=shape[3])
            return ap

    sp.op(mk("dma_start", out=cst[:], in_=cst_d), writes=[B_const])
    sp.op(mk("dma_start", out=ident_bf[:], in_=identbf_d), writes=[B_const])
    sp.op(mk("dma_start", out=ident_f[:], in_=identf_d), writes=[B_const])
    sp.op(mk("dma_start", out=tri_bf[:], in_=tribf_d), writes=[B_const])
    pool.op(mk("memset", state[:], 0.0), writes=[B_state])
    pool.op(mk("memset", halo[:], 0.0), writes=[B_halo])
    for l in range(depth):
        sp.op(mk("dma_start", out=subg_bc[:, l, :], in_=subg[l:l + 1, :].partition_broadcast(128)[:, 0, :]), writes=[B_lay])
    P.barrier()
    cv = Carver()
    lq = cv.take([128, depth, 4, 64], F32)
    lj = cv.take([128, 64], F32)
    for l in range(depth):
        sp.op(mk("dma_start", out=lq[:, l, :, :], in_=lamv[l:l + 1, :, :].partition_broadcast(128)[:, 0, :, :]), writes=[B_lay])
    for l in range(depth):
        lam_init = 0.8 - 0.6 * math.exp(-0.3 * l)
        for k in range(2):
            dve.op(mk("tensor_tensor", out=lj, in0=lq[:, l, 2 * k, :], in1=lq[:, l, 2 * k + 1, :], op=ALU.mult), reads=[B_lay], writes=[B_lay])
            dve.op(mk("tensor_reduce", out=lam_t[:, l, 1 + k:2 + k], in_=lj, op=ALU.add, axis=mybir.AxisListType.X), reads=[B_lay], writes=[B_lay])
        act.op(mk("activation", out=lam_t[:, l, 3:5], in_=lam_t[:, l, 1:3], func=AF.Exp), reads=[B_lay], writes=[B_lay])
        dve.op(mk("tensor_tensor", out=lam_t[:, l, 5:6], in0=lam_t[:, l, 4:5], in1=lam_t[:, l, 3:4], op=ALU.subtract), reads=[B_lay], writes=[B_lay])
        dve.op(mk("tensor_scalar", out=lam_t[:, l, 0:1], in0=lam_t[:, l, 5:6], scalar1=-lam_init, scalar2=None, op0=ALU.add), reads=[B_lay], writes=[B_lay])
        dve.op(mk("tensor_scalar", out=subg_bc[:, l, :], in0=subg_bc[:, l, :], scalar1=1.0 - lam_init, scalar2=None, op0=ALU.mult), reads=[B_lay], writes=[B_lay])
    cwl = cv.take([128, depth * 4, 128], F32)
    for l in range(depth):
        for j in range(4):
            sp.op(mk("dma_start", out=cwl[0:2 * NFC, l * 4 + j, :], in_=convw[l, j, :, :]), writes=[B_cw])
    for l in range(depth):
        for j in range(4):
            ps, bps = next_ps("tp")
            pe.op(mk("transpose", out=ps[:, 0:2 * NFC], in_=cwl[0:2 * NFC, l * 4 + j, :], identity=ident_f[0:2 * NFC, 0:2 * NFC]),
                  reads=[B_cw, B_const], writes=[bps])
            dve.op(mk("tensor_copy", out=cw[:, l, j, :], in_=ps[:, 0:2 * NFC]), reads=[bps], writes=[B_lay])
    P.barrier()

    units = []
    unit_state = {"emitted": 0, "stq": 0}
    cast_eng_cycle = [act, dve, act, dve]
    pending_casts = []

    def emit_unit_load(ui):
        u = units[ui]
        slot = ui % 2
        cidx = ui % NUP
        if ui >= NUP:
            sp.op(mk("dma_start", out=wbf[slot][:, :], in_=wcache_l[cidx // UPL][cidx % UPL, :, :]), reads=[B_wc[cidx]], writes=B_wbf[slot])
            return
        nq = len(u["quarters"])
        for qi, (src, ne, shp) in enumerate(u["quarters"]):
            si = unit_state["stq"] % NST
            unit_state["stq"] += 1
            st = wst[si]
            if len(shp) == 2:
                dst = st[:, 0:ne].rearrange("p (a b) -> p a b", b=shp[1])
            else:
                dst = st[:, 0:ne]
            sp.op(mk("dma_start", out=dst, in_=src), writes=[B_wst[si]])
            ce = cast_eng_cycle[qi % 4]
            o = wbf[slot][:, qi * ne: qi * ne + ne]
            i_ = st[:, 0:ne]

            def do_cast(ce=ce, o=o, i_=i_, si=si, slot=slot, qi=qi, nq=nq, cidx=cidx):
                if ce is act:
                    ce.op(mk("copy", out=o, in_=i_), reads=[B_wst[si]], writes=[B_wbf[slot][qi]], after=B_wbf[slot])
                else:
                    ce.op(mk("tensor_copy", out=o, in_=i_), reads=[B_wst[si]], writes=[B_wbf[slot][qi]], after=B_wbf[slot])
                if qi == nq - 1 and len(tiles) > 1:
                    sp.op(mk("dma_start", out=wcache_l[cidx // UPL][cidx % UPL, :, :], in_=wbf[slot][:, :]), reads=B_wbf[slot], writes=[B_wc[cidx]])
            pending_casts.append((ui, do_cast))

    def pump(n=1):
        for _ in range(n):
            if pending_casts:
                pending_casts.pop(0)[1]()

    def get_unit(ui):
        def flush():
            while pending_casts and pending_casts[0][0] <= ui:
                pending_casts.pop(0)[1]()
        flush()
        while unit_state["emitted"] <= ui:
            emit_unit_load(unit_state["emitted"])
            unit_state["emitted"] += 1
            flush()
        if unit_state["emitted"] == ui + 1 and ui + 1 < len(units):
            emit_unit_load(ui + 1)
            unit_state["emitted"] += 1
        return wbf[ui % 2], B_wbf[ui % 2]

    def unit_cols(wmat, l, col0, ncol):
        qs = []
        for q in range(4):
            src = wmat[l, q * 512:(q + 1) * 512, col0:col0 + ncol].rearrange("(k p) c -> p k c", p=128)
            qs.append((src, 4 * ncol, (4, ncol)))
        return {"quarters": qs, "kind": "cols", "ncol": ncol}

    def unit_rows(wmat, l, row0, nf):
        qs = []
        for f in range(nf):
            src = wmat[l, row0 + f * 128: row0 + (f + 1) * 128, :]
            qs.append((src, 2048, (2048,)))
        return {"quarters": qs, "kind": "rows", "nf": nf}

    ngrp = (NFC + 3) // 4
    for ti in range(len(tiles)):
        for l in range(depth):
            for u in range(32):
                units.append(unit_cols(w_in, l, u * 512, 512))
            for u in range(4):
                units.append(unit_cols(w_a, l, u * 512, 512))
            for u in range(4):
                units.append(unit_cols(w_b, l, u * 512, 512))
            for u in range(4):
                units.append(unit_cols(w_o, l, u * 512, 512))
            for g in range(ngrp):
                nf = min(4, NFC - 4 * g)
                units.append(unit_cols(w_up, l, g * 512, nf * 128))
                units.append(unit_cols(w_up, l, DFF + g * 512, nf * 128))
                if g > 0:
                    units.append(unit_rows(w_dn, l, (g - 1) * 512, 4))
            units.append(unit_rows(w_dn, l, (ngrp - 1) * 512, NFC - 4 * (ngrp - 1)))
    ucur = [0]

    def take_unit():
        ui = ucur[0]
        ucur[0] += 1
        t, b = get_unit(ui)
        return units[ui], t, b

    def gemm_tok(AT, B_AT, T, NT, W, B_W, ncol, epi):
        Wv = W[:, 0:16 * ncol].rearrange("p (k c) -> p k c", c=ncol)
        tails = []
        for c in range(NT):
            ps, bps = next_ps()
            for kc in range(16):
                pe.op(mk("matmul", ps[:, 0:ncol], lhsT=AT[:, kc, c * 128:(c + 1) * 128], rhs=Wv[:, kc, :],
                                                            start=(kc == 0), stop=(kc == 15)),
                      reads=[B_AT, B_W[kc // 4]], writes=[bps], signal=(kc == 15))
            pump(1)
            tl = epi(c, ps, bps)
            if tl is not None:
                tails.append(tl)
            if len(tails) > 2:
                tails.pop(0)()
        while tails:
            tails.pop(0)()

    def transpose_cols(src, B_src, nblk, dst_fn, B_dst, evac_engs):
        j = 0
        k = 0
        while j < nblk:
            n = min(8, nblk - j)
            ps, bps = next_ps("tp")
            pv = ps[:].bitcast(BF16).rearrange("p (a b) -> p a b", b=128)
            for jj in range(n):
                pe.op(mk("transpose", out=pv[:, jj, :], in_=src[:, (j + jj) * 128:(j + jj + 1) * 128], identity=ident_bf[:]),
                      reads=[B_src, B_const], writes=[bps], signal=(jj == n - 1))
            ee = evac_engs[k % len(evac_engs)]
            k += 1
            d = dst_fn(j, n)
            if ee is act:
                ee.op(mk("copy", out=d, in_=pv[:, 0:n, :]), reads=[bps], writes=[B_dst])
            else:
                ee.op(mk("tensor_copy", out=d, in_=pv[:, 0:n, :]), reads=[bps], writes=[B_dst])
            j += n

    def load_ln_params(cvr, row_g, row_b):
        g = cvr.take([128, D], F32)
        b = cvr.take([128, D], F32)
        B = Buf("lnp")
        sp.op(mk("dma_start", out=g, in_=lnp[row_g:row_g + 1, :].partition_broadcast(128)[:, 0, :]), writes=[B])
        sp.op(mk("dma_start", out=b, in_=lnp[row_b:row_b + 1, :].partition_broadcast(128)[:, 0, :]), writes=[B])
        return g, b, B

    def layer_norm_chunk(c, g, b, Bp, stats, mv, Bs):
        hc = h[:, c, :]
        for s in range(4):
            dve.op(mk("bn_stats", out=stats[:, s, :], in_=h[:, c, s * 512:(s + 1) * 512]), reads=[B_h[c]], writes=[Bs])
        dve.op(mk("bn_aggr", out=mv[:, 0:2], in_=stats.rearrange("p a b -> p (a b)")), reads=[Bs], writes=[Bs])
        act.op(mk("activation", out=mv[:, 2:3], in_=mv[:, 1:2], func=AF.Sqrt, bias=cst[:, CST_EPS:CST_EPS + 1], scale=1.0), reads=[Bs, B_const], writes=[Bs])
        dve.op(mk("reciprocal", out=mv[:, 3:4], in_=mv[:, 2:3]), reads=[Bs], writes=[Bs])
        dve.op(mk("scalar_tensor_tensor", out=hc, in0=hc, scalar=mv[:, 0:1], in1=g, op0=ALU.subtract, op1=ALU.mult), reads=[Bs, Bp, B_h[c]], writes=[B_h[c]])
        dve.op(mk("scalar_tensor_tensor", out=hc, in0=hc, scalar=mv[:, 3:4], in1=b, op0=ALU.mult, op1=ALU.add), reads=[Bs, Bp, B_h[c]], writes=[B_h[c]])

    def layer_norm_tile(NT, g, b, Bp, cvr):
        st = [cvr.take([128, 4, 6], F32) for _ in range(NT)]
        mvs = [cvr.take([128, 8], F32) for _ in range(NT)]
        Bsc = [Buf(f"lnst{c}") for c in range(NT)]
        for c in range(NT):
            for s_ in range(4):
                dve.op(mk("bn_stats", out=st[c][:, s_, :], in_=h[:, c, s_ * 512:(s_ + 1) * 512]), reads=[B_h[c]], writes=[Bsc[c]])
            dve.op(mk("bn_aggr", out=mvs[c][:, 0:2], in_=st[c].rearrange("p a b -> p (a b)")), reads=[Bsc[c]], writes=[Bsc[c]])
        for c in range(NT):
            act.op(mk("activation", out=mvs[c][:, 2:3], in_=mvs[c][:, 1:2], func=AF.Sqrt, bias=cst[:, CST_EPS:CST_EPS + 1], scale=1.0), reads=[Bsc[c], B_const], writes=[Bsc[c]])
        for c in range(NT):
            hc = h[:, c, :]
            dve.op(mk("reciprocal", out=mvs[c][:, 3:4], in_=mvs[c][:, 2:3]), reads=[Bsc[c]], writes=[Bsc[c]])
            dve.op(mk("scalar_tensor_tensor", out=hc, in0=hc, scalar=mvs[c][:, 0:1], in1=g, op0=ALU.subtract, op1=ALU.mult), reads=[Bsc[c], Bp, B_h[c]], writes=[B_h[c]])
            dve.op(mk("scalar_tensor_tensor", out=hc, in0=hc, scalar=mvs[c][:, 3:4], in1=b, op0=ALU.mult, op1=ALU.add), reads=[Bsc[c], Bp, B_h[c]], writes=[B_h[c]])

    def make_T_from_h(cvr, NT, T, dstT, B_dst):
        hb = [cvr.take([128, D], BF16) for _ in range(2)]
        Bhb = [Buf("hb0"), Buf("hb1")]
        for c in range(NT):
            k = c % 2
            act.op(mk("copy", out=hb[k], in_=h[:, c, :]), reads=[B_h[c]], writes=[Bhb[k]])
            transpose_cols(hb[k], Bhb[k], 16, lambda j0, n, c=c: dstT[:, j0:j0 + n, c * 128:(c + 1) * 128], B_dst, [dve, act])

    def dbg_dump(name, src_ap, rows, reads):
        if name in dbg:
            sp.op(mk("dma_start", out=dbg[name][rows], in_=src_ap), reads=reads)

    def main_loop():
      c0 = 0
      checkpoint(0)
      for ti, NT in enumerate(tiles):
          T = NT * 128
          nkc_tile = c0 + NT
          Lk = nkc_tile * 128
          AT = bufA[:, 0:16 * T].rearrange("p (k t) -> p k t", t=T)
          BT = bufB[:, 0:16 * T].rearrange("p (k t) -> p k t", t=T)
          A_tok = bufA[:, 0:NT * D].rearrange("p (c d) -> p c d", d=D)

          P.barrier()
          cv = Carver()
          for c in range(NT):
              sp.op(mk("dma_start", out=h[:, c, :], in_=xin[(c0 + c) * 128:(c0 + c + 1) * 128, :]), writes=[B_h[c]])
          sp.op(mk("dma_start", out=rope_t[:, 0:NT, :], in_=rope_d[c0 * 128:(c0 + NT) * 128, :].rearrange("(c p) f -> p c f", p=128)), writes=[B_rope])
          g_, b_, Bp = load_ln_params(cv, 0, 1)
          stats = cv.take([128, 4, 6], F32)
          mv = cv.take([128, 8], F32)
          Bs = Buf("lnstat")
          layer_norm_tile(NT, g_, b_, Bp, cv)
          make_T_from_h(cv, NT, T, AT, B_A)
          if ti == 0:
              for c in range(NT):
                  dbg_dump("h0", h[:, c, :], (slice(c * 128, (c + 1) * 128), slice(None)), [B_h[c]])

          for l in range(depth):
              P.barrier()
              checkpoint(1)
              cv = Carver()
              stage = [cv.take([128, NT, 512], BF16) for _ in range(2)]
              B_stage = [Buf("stage0"), Buf("stage1")]
              stageT = [cv.take([128, 4, T], BF16) for _ in range(2)]
              B_stageT = [Buf("stageT0"), Buf("stageT1")]
              stageV = [cv.take([128, 4, NT, 129], BF16) for _ in range(2)]
              B_stageV = [Buf("stageV0"), Buf("stageV1")]
              rA = [cv.take([128, 512], F32)] * 2
              rB = [cv.take([128, 512], F32)] * 2
              B_r = [Buf("r0")] * 2
              rk_cnt = [0]

              def rope_epi(fam, ub, c, ps, bps, k):
                  if fam in ("aq", "ak"):
                      half, nb, co, so = 32, 8, 0, 32
                  else:
                      half, nb, co, so = 64, 4, 64, 128
                  j = rk_cnt[0] % 2
                  rk_cnt[0] += 1
                  psv = ps[:].rearrange("p (b two x) -> p b two x", two=2, x=half)
                  Av = rA[j].rearrange("p (b two x) -> p b two x", two=2, x=half)
                  Bv = rB[j].rearrange("p (b two x) -> p b two x", two=2, x=half)
                  cosb = rope_t[:, c, co:co + half].unsqueeze(1).unsqueeze(1).broadcast_to([128, nb, 2, half])
                  sinb = rope_t[:, c, so:so + half].unsqueeze(1).broadcast_to([128, nb, half])
                  dve.op(mk("tensor_tensor", out=Av, in0=psv, in1=cosb, op=ALU.mult), reads=[bps, B_rope], writes=[B_r[j]])
                  dve.op(mk("tensor_tensor", out=Bv[:, :, 0, :], in0=psv[:, :, 1, :], in1=sinb, op=ALU.mult), reads=[bps, B_rope], writes=[B_r[j]])
                  dve.op(mk("tensor_tensor", out=Bv[:, :, 1, :], in0=psv[:, :, 0, :], in1=sinb, op=ALU.mult), reads=[bps, B_rope], writes=[B_r[j]])
                  if fam in ("aq", "ak"):
                      sv = stage[k][:, c, :].rearrange("p (b two x) -> p b two x", two=2, x=half)
                      dve.op(mk("tensor_tensor", out=sv[:, :, 0, :], in0=Av[:, :, 0, :], in1=Bv[:, :, 0, :], op=ALU.subtract), reads=[B_r[j]], writes=[B_stage[k]])
                      dve.op(mk("tensor_tensor", out=sv[:, :, 1, :], in0=Av[:, :, 1, :], in1=Bv[:, :, 1, :], op=ALU.add), reads=[B_r[j]], writes=[B_stage[k]])
                  else:
                      dve.op(mk("tensor_tensor", out=Av[:, :, 0, :], in0=Av[:, :, 0, :], in1=Bv[:, :, 0, :], op=ALU.subtract), reads=[B_r[j]], writes=[B_r[j]])
                      dve.op(mk("tensor_tensor", out=Av[:, :, 1, :], in0=Av[:, :, 1, :], in1=Bv[:, :, 1, :], op=ALU.add), reads=[B_r[j]], writes=[B_r[j]])
                      if fam == "rq":
                          base = CST_QDEC
                      else:
                          base = CST_KINV0 if (c0 + c) == 0 else CST_KINV
                      scl = cst[:, base + ub * 4: base + ub * 4 + 4].unsqueeze(2).broadcast_to([128, 4, 128])
                      a3 = rA[j].rearrange("p (b x) -> p b x", x=128)
                      s3 = stage[k][:, c, :].rearrange("p (b x) -> p b x", x=128)
                      dve.op(mk("tensor_tensor", out=s3, in0=a3, in1=scl, op=ALU.mult), reads=[B_r[j], B_const], writes=[B_stage[k]])

              ucount = [0]

              def run_unit(fam, ub):
                  k = ucount[0] % 2
                  ucount[0] += 1
                  uinfo, W, B_W = take_unit()

                  if fam in ("aq", "ak", "rq", "rk"):
                      def epi(c, ps, bps):
                          rope_epi(fam, ub, c, ps, bps, k)
                          return lambda c=c: transpose_cols(stage[k][:, c, :], B_stage[k], 4, lambda j0, n, c=c: stageT[k][:, j0:j0 + n, c * 128:(c + 1) * 128], B_stageT[k], [act])
                  elif fam == "av":
                      def epi(c, ps, bps):
                          act.op(mk("copy", out=stageV[k][:, :, c, 0:128], in_=ps[:].rearrange("p (a x) -> p a x", x=128)), reads=[bps], writes=[B_stageV[k]])
                          if c == 0:
                              pool.op(mk("memset", stageV[k][:, :, :, 128:129], 1.0), writes=[B_stageV[k]])
                          if (c0 + c) == 0:
                              pool.op(mk("tensor_scalar", out=stageV[k][:, :, 0, :], in0=stageV[k][:, :, 0, :], scalar1=cst[:, CST_VALID:CST_VALID + 1], scalar2=None, op0=ALU.mult),
                                      reads=[B_const, B_stageV[k]], writes=[B_stageV[k]])
                  elif fam == "rv":
                      def epi(c, ps, bps):
                          act.op(mk("copy", out=stage[k][:, c, :], in_=ps[:]), reads=[bps], writes=[B_stage[k]])
                  elif fam == "rg":
                      def epi(c, ps, bps):
                          act.op(mk("activation", out=stage[k][:, c, :], in_=ps[:], func=AF.Silu), reads=[bps], writes=[B_stage[k]])
                  else:
                      def epi(c, ps, bps):
                          act.op(mk("activation", out=stage[k][:, c, :], in_=ps[:], func=AF.Sigmoid), reads=[bps], writes=[B_stage[k]])
                  gemm_tok(AT, B_A, T, NT, W, B_W, 512, epi)
                  if fam == "aq":
                      sp.op(mk("dma_start", out=qT_s[ub * 4:ub * 4 + 4, :, 0:T].rearrange("a p t -> p a t"), in_=stageT[k]), reads=[B_stageT[k]], writes=[B_qT])
                  elif fam == "ak":
                      sp.op(mk("dma_start", out=kT_c[l, ub * 4:ub * 4 + 4, :, c0 * 128:c0 * 128 + T].rearrange("a p t -> p a t"), in_=stageT[k]), reads=[B_stageT[k]], writes=[B_kT])
                  elif fam == "rq":
                      sp.op(mk("dma_start", out=qdT_s[ub * 4:ub * 4 + 4, :, 0:T].rearrange("a p t -> p a t"), in_=stageT[k]), reads=[B_stageT[k]], writes=[B_qdT])
                  elif fam == "rk":
                      sp.op(mk("dma_start", out=kiT_s[ub * 4:ub * 4 + 4, :, 0:T].rearrange("a p t -> p a t"), in_=stageT[k]), reads=[B_stageT[k]], writes=[B_kiT])
                      sp.op(mk("dma_start", out=ki_s[0:T, ub * 512:(ub + 1) * 512].rearrange("(c p) f -> p c f", p=128), in_=stage[k]), reads=[B_stage[k]], writes=[B_ki])
                  elif fam == "av":
                      sp.op(mk("dma_start", out=v_c[l, ub * 4:ub * 4 + 4, :, c0 * 129:(c0 + NT) * 129].rearrange("a p x -> p a x"),
                                                  in_=stageV[k].rearrange("p a c x -> p a (c x)")), reads=[B_stageV[k]], writes=[B_v])
                  else:
                      dst = {"rv": (rv_s, B_rv), "rg": (sg_s, B_sg), "ga": (sga_s, B_sga), "gb": (sgb_s, B_sgb)}[fam]
                      sp.op(mk("dma_start", out=dst[0][0:T, ub * 512:(ub + 1) * 512].rearrange("(c p) f -> p c f", p=128), in_=stage[k]), reads=[B_stage[k]], writes=[dst[1]])

              fams = [("aq", 4), ("ak", 4), ("av", 4), ("rq", 2), ("rk", 2), ("rv", 4), ("rg", 4), ("ga", 4), ("gb", 4)]
              for fam, n in fams:
                  for ub in range(n):
                      run_unit(fam, ub)

              P.barrier()
              checkpoint(2)
              cv = Carver()
              qTh0 = [cv.take([128, T], BF16) for _ in range(2)]
              qTh1 = [cv.take([128, T], BF16) for _ in range(2)]
              kTh = [cv.take([128, Lk], BF16) for _ in range(2)]
              Vh = [cv.take([128, nkc_tile, 129], BF16) for _ in range(2)]
              B_ld = [Buf("attld0"), Buf("attld1")]
              for k_ in range(2):
                  pool.op(mk("memset", qTh0[k_][64:128, :], 0.0), writes=[B_ld[k_]])
                  pool.op(mk("memset", qTh1[k_][0:64, :], 0.0), writes=[B_ld[k_]])
              NE = 3
              Et = [cv.take([128, 512], BF16) for _ in range(NE)]
              B_E = [Buf(f"E{i}") for i in range(NE)]
              junk = cv.take([128, 128], F32)
              ecnt = [0]
              ocnt = [0]
              ps_groups["acc"] = [0, 1]
              ps_groups["mm"] = [2, 3, 4, 5]
              ps_rr["acc"] = 0
              ps_rr["mm"] = 0
              items = []
              for a in range(16):
                  for i in range(NT):
                      gi = c0 + i
                      for m in range(2):
                          g0s = list(range(0, gi + 1, 4))
                          for g0 in g0s:
                              items.append((a, i, m, g0, g0 == g0s[-1]))
              loaded = set()
              Ocur = {}

              def emit_S(it):
                  a, i, m, g0, last = it
                  k = a % 2
                  gi = c0 + i
                  if a not in loaded:
                      loaded.add(a)
                      sp.op(mk("dma_start", out=qTh0[k][0:64, :], in_=qT_s[a, 0:64, 0:T]), reads=[B_qT], writes=[B_ld[k]])
                      sp.op(mk("dma_start", out=qTh1[k][64:128, :], in_=qT_s[a, 64:128, 0:T]), reads=[B_qT], writes=[B_ld[k]])
                      sp.op(mk("dma_start", out=kTh[k], in_=kT_c[l, a, :, 0:Lk]), reads=[B_kT], writes=[B_ld[k]])
                      sp.op(mk("dma_start", out=Vh[k].rearrange("p c x -> p (c x)"), in_=v_c[l, a, :, 0:nkc_tile * 129]), reads=[B_v], writes=[B_ld[k]])
                  pr = slice(m * 64, (m + 1) * 64)
                  js = list(range(g0, min(g0 + 4, gi + 1)))
                  S, bS = next_ps("mm")
                  for jj, j in enumerate(js):
                      pe.op(mk("matmul", S[:, jj * 128:(jj + 1) * 128], lhsT=kTh[k][:, j * 128:(j + 1) * 128],
                               rhs=(qTh0 if m == 0 else qTh1)[k][:, i * 128:(i + 1) * 128], start=True, stop=True),
                            reads=[B_ld[k]], writes=[bS], signal=(jj == len(js) - 1))
                  return (S, bS, js)

              def emit_rest(it, st):
                  a, i, m, g0, last = it
                  S, bS, js = st
                  k = a % 2
                  gi = c0 + i
                  if g0 == 0:
                      Ocur[m] = next_ps("acc")
                  O, bO = Ocur[m]
                  ei = ecnt[0] % NE
                  ecnt[0] += 1
                  E, bE = Et[ei], B_E[ei]
                  n = len(js)
                  act.op(mk("activation", out=E[:, 0:n * 128], in_=S[:, 0:n * 128], func=AF.Exp, scale=0.125), reads=[bS], writes=[bE])
                  if gi in js:
                      jj = js.index(gi)
                      dve.op(mk("tensor_tensor", out=E[:, jj * 128:(jj + 1) * 128], in0=E[:, jj * 128:(jj + 1) * 128], in1=tri_bf[:], op=ALU.mult),
                             reads=[bE, B_const], writes=[bE])
                  for jj, j in enumerate(js):
                      pe.op(mk("matmul", O[:, 0:129], lhsT=E[:, jj * 128:(jj + 1) * 128], rhs=Vh[k][:, j, :],
                               start=(j == 0), stop=(j == gi)),
                            reads=[bE, B_ld[k]], writes=[bO], signal=(jj == len(js) - 1))
                  if not (last and m == 1):
                      return
                  oi = ocnt[0] % NOR
                  ocnt[0] += 1
                  (O0, bO0), (O1, bO1) = Ocur[0], Ocur[1]
                  s_ = sc[oi]
                  o_ = oA[oi]
                  Bo = B_o[oi]
                  dve.op(mk("tensor_scalar", out=s_[:, 0:1], in0=O0[:, 128:129], scalar1=cst[:, CST_TINY:CST_TINY + 1], scalar2=None, op0=ALU.max), reads=[bO0, B_const], writes=[Bo])
                  dve.op(mk("tensor_scalar", out=s_[:, 1:2], in0=O1[:, 128:129], scalar1=cst[:, CST_TINY:CST_TINY + 1], scalar2=None, op0=ALU.max), reads=[bO1, B_const], writes=[Bo])
                  dve.op(mk("reciprocal", out=s_[:, 2:4], in_=s_[:, 0:2]), reads=[Bo], writes=[Bo])
                  dve.op(mk("tensor_tensor", out=s_[:, 4:5], in0=s_[:, 3:4], in1=lam_t[:, l, 0:1], op=ALU.mult), reads=[Bo, B_lay], writes=[Bo])
                  dve.op(mk("tensor_scalar", out=o_, in0=O0[:, 0:128], scalar1=s_[:, 2:3], scalar2=None, op0=ALU.mult), reads=[bO0, Bo], writes=[Bo])
                  dve.op(mk("scalar_tensor_tensor", out=o_, in0=O1[:, 0:128], scalar=s_[:, 4:5], in1=o_, op0=ALU.mult, op1=ALU.add), reads=[bO1, Bo], writes=[Bo])
                  dve.op(mk("scalar_tensor_tensor", out=junk, in0=o_, scalar=1.0, in1=o_, op0=ALU.mult, op1=ALU.mult, accum_out=s_[:, 5:6]), reads=[Bo], writes=[Bo, B_junk])
                  y_ = Yb[oi]

                  def part2(s_=s_, o_=o_, y_=y_, Bo=Bo):
                      pool.op(mk("tensor_scalar", out=s_[:, 6:7], in0=s_[:, 5:6], scalar1=1.0 / 128.0, scalar2=EPS, op0=ALU.mult, op1=ALU.add), reads=[Bo], writes=[Bo])
                      pool.op(mk("tensor_tensor", out=s_[:, 7:8], in0=s_[:, 6:7], in1=cst[:, CST_NHALF:CST_NHALF + 1], op=ALU.pow), reads=[Bo, B_const], writes=[Bo])
                      dve.op(mk("scalar_tensor_tensor", out=y_, in0=o_, scalar=s_[:, 7:8], in1=subg_bc[:, l, :], op0=ALU.mult, op1=ALU.mult), reads=[Bo, B_lay], writes=[Bo])

                  def part3(y_=y_, Bo=Bo, a=a, i=i):
                      transpose_cols(y_, Bo, 1, lambda j0, n, a=a, i=i: BT[:, a:a + 1, i * 128:(i + 1) * 128], B_B, [dve])
                  etails.append([2, part2])
                  etails.append([4, part3])

              def run_tails(force=False):
                  keep = []
                  for t in etails:
                      t[0] -= 1
                  for t in list(etails):
                      if force or t[0] <= 0:
                          t[1]()
                          etails.remove(t)

              etails = []
              NOR = 6
              oA = [cv.take([128, 128], F32) for _ in range(NOR)]
              Yb = [cv.take([128, 128], BF16) for _ in range(NOR)]
              sc = [cv.take([128, 8], F32) for _ in range(NOR)]
              B_o = [Buf(f"o{x}") for x in range(NOR)]
              B_junk = Buf("junk")
              DEPTH_PIPE = 3
              pipe = []
              for it in items:
                  pipe.append((it, emit_S(it)))
                  if len(pipe) > DEPTH_PIPE:
                      emit_rest(*pipe.pop(0))
                      run_tails()
              while pipe:
                  emit_rest(*pipe.pop(0))
                  run_tails()
              while etails:
                  run_tails(force=True)
              ps_groups["acc"] = [0, 1, 2, 3]
              ps_groups["mm"] = [4, 5]
              if "oaT" in dbg and ti == 0 and l == 0:
                  P.barrier()
                  tmpf = cv.take([128, T], F32)
                  for kc in range(16):
                      Bt = Buf("t")
                      dve.op(mk("tensor_copy", out=tmpf, in_=BT[:, kc, :]), reads=[B_B], writes=[Bt])
                      sp.op(mk("dma_start", out=dbg["oaT"][kc * 128:(kc + 1) * 128, 0:T], in_=tmpf), reads=[Bt])
                      P.barrier()

              P.barrier()
              checkpoint(3)
              cv = Carver()
              gst = [cv.take([128, NT, 512], BF16) for _ in range(2)]
              B_gst = [Buf("gst0"), Buf("gst1")]
              tmpf = [cv.take([128, 512], F32) for _ in range(2)]
              B_tmpf = [Buf("tmpf0"), Buf("tmpf1")]
              for u in range(4):
                  k = u % 2
                  uinfo, W, B_W = take_unit()
                  sp.op(mk("dma_start", out=gst[k], in_=sga_s[0:T, u * 512:(u + 1) * 512].rearrange("(c p) f -> p c f", p=128)), reads=[B_sga], writes=[B_gst[k]])

                  def epi(c, ps, bps, u=u, k=k):
                      dve.op(mk("tensor_tensor", out=A_tok[:, c, u * 512:(u + 1) * 512], in0=ps[:], in1=gst[k][:, c, :], op=ALU.mult), reads=[bps, B_gst[k]], writes=[B_A])
                  gemm_tok(BT, B_B, T, NT, W, B_W, 512, epi)

              P.barrier()
              checkpoint(4)
              cv = Carver()
              retg_bc = cv.take([128, D], F32)
              sp.op(mk("dma_start", out=retg_bc, in_=retg[l:l + 1, :].partition_broadcast(128)[:, 0, :]), writes=[B_lay])
              qdh = [cv.take([128, T], BF16) for _ in range(2)]
              kih = [cv.take([128, T], BF16) for _ in range(2)]
              kth = [cv.take([128, NT, 128], BF16) for _ in range(2)]
              rvh = [cv.take([128, NT, 256], BF16) for _ in range(2)]
              sgh = [cv.take([128, NT, 256], BF16) for _ in range(2)]
              B_rl = [Buf("rl0"), Buf("rl1")]
              Sbf = cv.take([128, 256], BF16)
              B_Sbf = Buf("Sbf")
              NRR = 4
              Pm = [cv.take([128, 128], BF16) for _ in range(NRR)]
              B_Pm = [Buf(f"Pm{x}") for x in range(NRR)]
              gs = [cv.take([128, 16], F32) for _ in range(NRR)]
              yf = [cv.take([128, 256], F32) for _ in range(NRR)]
              yb = [cv.take([128, 256], BF16) for _ in range(NRR)]
              B_y = [Buf(f"y{x}") for x in range(NRR)]
              rc = [0]
              rtails = []
              for r in range(8):
                  k = r % 2
                  sp.op(mk("dma_start", out=qdh[k], in_=qdT_s[r, :, 0:T]), reads=[B_qdT], writes=[B_rl[k]])
                  sp.op(mk("dma_start", out=kih[k], in_=kiT_s[r, :, 0:T]), reads=[B_kiT], writes=[B_rl[k]])
                  sp.op(mk("dma_start", out=kth[k], in_=ki_s[0:T, r * 128:(r + 1) * 128].rearrange("(c p) f -> p c f", p=128)), reads=[B_ki], writes=[B_rl[k]])
                  sp.op(mk("dma_start", out=rvh[k], in_=rv_s[0:T, r * 256:(r + 1) * 256].rearrange("(c p) f -> p c f", p=128)), reads=[B_rv], writes=[B_rl[k]])
                  sp.op(mk("dma_start", out=sgh[k], in_=sg_s[0:T, r * 256:(r + 1) * 256].rearrange("(c p) f -> p c f", p=128)), reads=[B_sg], writes=[B_rl[k]])
                  St = state[:, l, r, :]
                  dve.op(mk("tensor_copy", out=Sbf, in_=St), reads=[B_state], writes=[B_Sbf])
                  for i in range(NT):
                      j = rc[0] % NRR
                      rc[0] += 1
                      cs = slice(i * 128, (i + 1) * 128)
                      S, bS = next_ps("mm")
                      pe.op(mk("matmul", S[:, 0:128], lhsT=kih[k][:, cs], rhs=qdh[k][:, cs], start=True, stop=True), reads=[B_rl[k]], writes=[bS])
                      dve.op(mk("tensor_tensor", out=Pm[j], in0=S[:, 0:128], in1=tri_bf[:], op=ALU.mult), reads=[bS, B_const], writes=[B_Pm[j]])
                      O, bO = next_ps()
                      pe.op(mk("matmul", O[:, 0:256], lhsT=Pm[j], rhs=rvh[k][:, i, :], start=True, stop=False), reads=[B_Pm[j], B_rl[k]], writes=[bO], signal=False)
                      pe.op(mk("matmul", O[:, 0:256], lhsT=qdh[k][:, cs], rhs=Sbf, start=False, stop=True), reads=[B_Sbf, B_rl[k]], writes=[bO])
                      U, bU = next_ps()
                      pe.op(mk("matmul", U[:, 0:256], lhsT=kth[k][:, i, :], rhs=rvh[k][:, i, :], start=True, stop=True), reads=[B_rl[k]], writes=[bU])
                      gcol = cst[:, CST_G128 + r:CST_G128 + r + 1]
                      dve.op(mk("tensor_scalar", out=St, in0=St, scalar1=gcol, scalar2=None, op0=ALU.mult), reads=[B_state, B_const], writes=[B_state])
                      dve.op(mk("scalar_tensor_tensor", out=St, in0=U[:, 0:256], scalar=gcol, in1=St, op0=ALU.mult, op1=ALU.add), reads=[bU, B_state, B_const], writes=[B_state])
                      dve.op(mk("tensor_copy", out=Sbf, in_=St), reads=[B_state], writes=[B_Sbf])
                      g_s = gs[j]
                      dve.op(mk("bn_stats", out=g_s[:, 0:6], in_=O[:, 0:256]), reads=[bO], writes=[B_y[j]])
                      dve.op(mk("bn_aggr", out=g_s[:, 6:8], in_=g_s[:, 0:6]), reads=[B_y[j]], writes=[B_y[j]])
                      act.op(mk("activation", out=g_s[:, 8:9], in_=g_s[:, 7:8], func=AF.Sqrt, bias=cst[:, CST_EPS:CST_EPS + 1], scale=1.0), reads=[B_y[j], B_const], writes=[B_y[j]])
                      dve.op(mk("reciprocal", out=g_s[:, 9:10], in_=g_s[:, 8:9]), reads=[B_y[j]], writes=[B_y[j]])
                      dve.op(mk("scalar_tensor_tensor", out=yf[j], in0=O[:, 0:256], scalar=g_s[:, 6:7], in1=retg_bc[:, r * 256:(r + 1) * 256], op0=ALU.subtract, op1=ALU.mult), reads=[bO, B_y[j], B_lay], writes=[B_y[j]])
                      dve.op(mk("scalar_tensor_tensor", out=yb[j], in0=yf[j], scalar=g_s[:, 9:10], in1=sgh[k][:, i, :], op0=ALU.mult, op1=ALU.mult), reads=[B_y[j], B_rl[k]], writes=[B_y[j]])
                      rtails.append(lambda j=j, r=r, i=i: transpose_cols(yb[j], B_y[j], 2, lambda j0, n, r=r, i=i: BT[:, 2 * r + j0:2 * r + j0 + n, i * 128:(i + 1) * 128], B_B, [act]))
                      if len(rtails) > 2:
                          rtails.pop(0)()
              while rtails:
                  rtails.pop(0)()

              P.barrier()
              checkpoint(5)
              cv = Carver()
              gst = [cv.take([128, NT, 512], BF16) for _ in range(2)]
              B_gst = [Buf("gst0"), Buf("gst1")]
              tmpf = [cv.take([128, 512], F32) for _ in range(2)]
              B_tmpf = [Buf("tmpf0"), Buf("tmpf1")]
              tc_ = [0]
              for u in range(4):
                  k = u % 2
                  uinfo, W, B_W = take_unit()
                  sp.op(mk("dma_start", out=gst[k], in_=sgb_s[0:T, u * 512:(u + 1) * 512].rearrange("(c p) f -> p c f", p=128)), reads=[B_sgb], writes=[B_gst[k]])

                  def epi(c, ps, bps, u=u, k=k):
                      j = tc_[0] % 2
                      tc_[0] += 1
                      dve.op(mk("tensor_tensor", out=tmpf[j], in0=ps[:], in1=gst[k][:, c, :], op=ALU.mult), reads=[bps, B_gst[k]], writes=[B_tmpf[j]])
                      pool.op(mk("tensor_tensor", out=A_tok[:, c, u * 512:(u + 1) * 512], in0=tmpf[j], in1=A_tok[:, c, u * 512:(u + 1) * 512], op=ALU.add), reads=[B_tmpf[j], B_A], writes=[B_A])
                  gemm_tok(BT, B_B, T, NT, W, B_W, 512, epi)
              P.barrier()
              checkpoint(6)
              for c in range(NT):
                  transpose_cols(A_tok[:, c, :], B_A, 16, lambda j0, n, c=c: BT[:, j0:j0 + n, c * 128:(c + 1) * 128], B_B, [dve, act])

              P.barrier()
              checkpoint(7)
              cv = Carver()
              g_, b_, Bp = load_ln_params(cv, 2 + 4 * l, 3 + 4 * l)
              stats = cv.take([128, 4, 6], F32)
              mv = cv.take([128, 8], F32)
              Bs = Buf("lnstat")
              for u in range(4):
                  uinfo, W, B_W = take_unit()

                  def epi(c, ps, bps, u=u):
                      hs = h[:, c, u * 512:(u + 1) * 512]
                      dve.op(mk("scalar_tensor_tensor", out=hs, in0=hs, scalar=alpha, in1=ps[:], op0=ALU.mult, op1=ALU.add), reads=[bps, B_h[c]], writes=[B_h[c]])
                  gemm_tok(BT, B_B, T, NT, W, B_W, 512, epi)
              layer_norm_tile(NT, g_, b_, Bp, cv)
              make_T_from_h(cv, NT, T, AT, B_A)
              if ti == 0 and l == 0:
                  for c in range(NT):
                      dbg_dump("h1", h[:, c, :], (slice(c * 128, (c + 1) * 128), slice(None)), [B_h[c]])
              for c in range(NT):
                  act.op(mk("activation", out=h[:, c, :], in_=h[:, c, :], func=AF.Copy, scale=alpha), reads=[B_h[c]], writes=[B_h[c]])

              P.barrier()
              checkpoint(8)
              cv = Carver()
              nblk = (T + 511) // 512
              bs = T // nblk
              assert bs * nblk == T
              actT = [cv.take([128, 4, T], BF16) for _ in range(2)]
              B_actT = [Buf("actT0"), Buf("actT1")]
              gl = cv.take([128, 4, T], F32)
              B_gl = Buf("gl")
              ug = [cv.take([128, T + 2], F32) for _ in range(2)]
              B_u = [Buf("u0"), Buf("u1")]
              t1 = [cv.take([128, T], F32) for _ in range(2)]
              B_t1 = [Buf("t10"), Buf("t11")]
              uc = [0]

              def emit_wdown(g):
                  nf = min(4, NFC - 4 * g)
                  ka = g % 2
                  _, Wd, B_Wd = take_unit()
                  Wdv = Wd[:, 0:nf * 2048].rearrange("p (f c) -> p f c", c=2048)
                  for c in range(NT):
                      pump(1)
                      for n in range(4):
                          ps, bps = next_ps("mm")
                          for f in range(nf):
                              pe.op(mk("matmul", ps[:], lhsT=actT[ka][:, f, c * 128:(c + 1) * 128], rhs=Wdv[:, f, n * 512:(n + 1) * 512],
                                       start=(f == 0), stop=(f == nf - 1)),
                                    reads=[B_actT[ka], B_Wd[f]], writes=[bps], signal=(f == nf - 1))
                          hs = h[:, c, n * 512:(n + 1) * 512]
                          dve.op(mk("tensor_tensor", out=hs, in0=hs, in1=ps[:], op=ALU.add), reads=[bps, B_h[c]], writes=[B_h[c]])

              for g in range(ngrp):
                  nf = min(4, NFC - 4 * g)
                  ka = g % 2
                  ncol = nf * 128
                  for w in range(2):
                      _, Wt, B_Wt = take_unit()
                      Wv_ = Wt[:, 0:16 * ncol].rearrange("p (k c) -> p k c", c=ncol)
                      for f in range(nf):
                          fc = g * 4 + f
                          ci = fc + w * NFC
                          j = uc[0] % 2
                          uc[0] += 1
                          for tb in range(nblk):
                              ps, bps = next_ps()
                              for kc in range(16):
                                  pe.op(mk("matmul", ps[:, 0:bs], lhsT=Wv_[:, kc, f * 128:(f + 1) * 128], rhs=AT[:, kc, tb * bs:(tb + 1) * bs],
                                           start=(kc == 0), stop=(kc == 15)),
                                        reads=[B_A, B_Wt[kc // 4]], writes=[bps], signal=(kc == 15))
                              act.op(mk("copy", out=ug[j][:, 2 + tb * bs:2 + (tb + 1) * bs], in_=ps[:, 0:bs]), reads=[bps], writes=[B_u[j]])
                          pump(1)
                          pool.op(mk("tensor_copy", out=ug[j][:, 0:2], in_=halo[:, l, ci, :]), reads=[B_halo], writes=[B_u[j]])
                          if ti == 0:
                              pool.op(mk("memset", ug[j][:, 2:2 + PAD], 0.0), writes=[B_u[j]])
                          pool.op(mk("tensor_copy", out=halo[:, l, ci, :], in_=ug[j][:, T:T + 2]), reads=[B_u[j]], writes=[B_halo])
                          dve.op(mk("tensor_scalar", out=t1[j], in0=ug[j][:, 0:T], scalar1=cw[:, l, 0, ci:ci + 1], scalar2=cw[:, l, 3, ci:ci + 1], op0=ALU.mult, op1=ALU.add),
                                 reads=[B_u[j], B_lay], writes=[B_t1[j]])
                          dve.op(mk("scalar_tensor_tensor", out=t1[j], in0=ug[j][:, 1:T + 1], scalar=cw[:, l, 1, ci:ci + 1], in1=t1[j], op0=ALU.mult, op1=ALU.add),
                                 reads=[B_u[j], B_lay, B_t1[j]], writes=[B_t1[j]])
                          dve.op(mk("scalar_tensor_tensor", out=t1[j], in0=ug[j][:, 2:T + 2], scalar=cw[:, l, 2, ci:ci + 1], in1=t1[j], op0=ALU.mult, op1=ALU.add),
                                 reads=[B_u[j], B_lay, B_t1[j]], writes=[B_t1[j]])
                          if w == 0:
                              act.op(mk("activation", out=gl[:, f, :], in_=t1[j], func=AF.Gelu), reads=[B_t1[j]], writes=[B_gl])
                          else:
                              pool.op(mk("tensor_tensor", out=actT[ka][:, f, :], in0=gl[:, f, :], in1=t1[j], op=ALU.mult), reads=[B_gl, B_t1[j]], writes=[B_actT[ka]])
                  if g > 0:
                      emit_wdown(g - 1)
              emit_wdown(ngrp - 1)

              P.barrier()
              checkpoint(9)
              if ti == 0 and l == 0:
                  for c in range(NT):
                      dbg_dump("h2p", h[:, c, :], (slice(c * 128, (c + 1) * 128), slice(None)), [B_h[c]])
              cv = Carver()
              g_, b_, Bp = load_ln_params(cv, 4 + 4 * l, 5 + 4 * l)
              stats = cv.take([128, 4, 6], F32)
              mv = cv.take([128, 8], F32)
              Bs = Buf("lnstat")
              layer_norm_tile(NT, g_, b_, Bp, cv)
              if l + 1 < depth:
                  make_T_from_h(cv, NT, T, AT, B_A)
              if ti == 0 and l == 0:
                  for c in range(NT):
                      dbg_dump("h2", h[:, c, :], (slice(c * 128, (c + 1) * 128), slice(None)), [B_h[c]])

          for c in range(NT):
              sp.op(mk("dma_start", out=out_d[(c0 + c) * 128:(c0 + c + 1) * 128, :], in_=h[:, c, :]), reads=[B_h[c]])
          c0 += NT


    try:
        main_loop()
        assert ucur[0] == len(units), (ucur[0], len(units))
    except StopBuild:
        pass
    P.finish()
    return nc


def prep_shared(inputs, depth):
    f = lambda a: np.ascontiguousarray(np.asarray(a, dtype=np.float32))
    lnp = np.concatenate([f(inputs["ln_emb_g"])[None], f(inputs["ln_emb_b"])[None]] +
                         [np.stack([f(inputs["ln1_g"])[l], f(inputs["ln1_b"])[l], f(inputs["ln2_g"])[l], f(inputs["ln2_b"])[l]]) for l in range(depth)], axis=0)
    lamv = np.stack([f(inputs["lam_q1"]), f(inputs["lam_k1"]), f(inputs["lam_q2"]), f(inputs["lam_k2"])], axis=1)
    convw = np.concatenate([f(inputs["conv_w"]), f(inputs["conv_b"])[:, None, :]], axis=1).reshape(depth, 4, 2 * NFC, 128)
    return {
        "w_in": f(inputs["w_in"]), "w_a": f(inputs["w_branch_a"]), "w_b": f(inputs["w_branch_b"]), "w_o": f(inputs["w_out"]),
        "w_up": f(inputs["w_up"]), "w_dn": f(inputs["w_down"]), "lnp": np.ascontiguousarray(lnp),
        "retg": f(inputs["ret_gn_g"]), "subg": f(inputs["subln_g"]), "lamv": np.ascontiguousarray(lamv), "convw": np.ascontiguousarray(convw),
    }


def run(inputs, tiles, depth, core_ids, debug=None, trace=False):
    x = np.asarray(inputs["x"], dtype=np.float32)
    B, S, _ = x.shape
    nch = (S + 128) // 128
    L = nch * 128
    nc = build_program(nch, tiles, depth, debug=debug)
    shared = prep_shared(inputs, depth)
    shared.update(make_consts(nch))
    meta = np.asarray(inputs["meta_tokens"], dtype=np.float32)
    in_maps = []
    for ci in core_ids:
        b = ci % B
        xin = np.zeros((L, D), np.float32)
        xin[PAD:PAD + N_META] = meta
        xin[128:] = x[b]
        m = dict(shared)
        m["xin"] = xin
        in_maps.append(m)
    res = run_bass_kernel_spmd(nc, in_maps, core_ids=list(core_ids), trace=trace)
    return res


def kernel(**inputs):
    depth = int(np.asarray(inputs["w_in"]).shape[0])
    x = np.asarray(inputs["x"])
    B, S, _ = x.shape
    nch = (S + 128) // 128
    tiles = []
    rem = nch
    while rem > 0:
        t = min(5 if not tiles else 4, rem)
        tiles.append(t)
        rem -= t
    res = run(inputs, tiles, depth, list(range(B)))
    out = np.stack([np.asarray(r["out"], dtype=np.float32)[128:] for r in res.results[:B]], axis=0)
    return out
```
